# Optimizing a Trainium2 kernel written in Bass

```python
import math
import jax
import jax.numpy as jnp
from jax import lax
import numpy as np

D_MODEL = 4096
BATCH = 1
SEQ = 16384
DEPTH = 2

CHUNK = 64
N_BRANCH = 4
D_MIX = D_MODEL // 4
RET_HEADS = 4
RET_DK = D_MIX // RET_HEADS
RET_DV = D_MIX // RET_HEADS
ROPE_BASE = 10000.0
LRU_BLOCKS = 8
LRU_BLOCK = D_MIX // LRU_BLOCKS
LRU_C = 8.0
CONV_WIDTH = 4
HG_HEADS = 8
HG_DK = D_MIX // HG_HEADS
HG_DV = D_MIX // HG_HEADS
DN_HEADS = 8
DN_DK = D_MIX // DN_HEADS
DN_DV = D_MIX // DN_HEADS
FFN_HIDDEN = (((8 * D_MODEL + 2) // 3 + 255) // 256) * 256
N_IN = 14 * D_MIX + 2 * DN_HEADS + N_BRANCH * D_MODEL
EPS = 1e-6

kernel_name = 'hybrid_stream_retention_rglru_hgrn2_gdn'


def _f32(t):
    return t.astype(jnp.float32)


def rmsnorm(x, g):
    x32 = _f32(x)
    y = x32 * lax.rsqrt(jnp.mean(x32 * x32, axis=-1, keepdims=True) + EPS)
    return (y * _f32(g)).astype(x.dtype)


def causal_depthwise_conv(x, w, b=None):
    c = x.shape[-1]
    y = lax.conv_general_dilated(x, w[:, None, :], window_strides=(1,), padding=[(CONV_WIDTH - 1, 0)],
                                 dimension_numbers=('NWC', 'WIO', 'NWC'), feature_group_count=c)
    if b is not None:
        y = y + b
    return y


def to_chunks(t):
    b, s, h, d = t.shape
    return t.reshape(b, s // CHUNK, CHUNK, h, d).transpose(0, 3, 1, 2, 4)


def from_chunks(t):
    b, h, n, c, d = t.shape
    return t.transpose(0, 2, 3, 1, 4).reshape(b, n * c, h, d)


def rotary(t):
    s, d = t.shape[1], t.shape[-1]
    half = d // 2
    inv_freq = 1.0 / (ROPE_BASE ** (jnp.arange(half, dtype=jnp.float32) / half))
    ang = jnp.arange(s, dtype=jnp.float32)[:, None] * inv_freq[None, :]
    cos = jnp.cos(ang)[None, :, None, :]
    sin = jnp.sin(ang)[None, :, None, :]
    t1, t2 = t[..., :half], t[..., half:]
    return jnp.concatenate([t1 * cos - t2 * sin, t1 * sin + t2 * cos], axis=-1)


def split_columns(u):
    sizes = [D_MIX] * 14 + [DN_HEADS, DN_HEADS, N_BRANCH * D_MODEL]
    parts = []
    off = 0
    for size in sizes:
        parts.append(u[..., off:off + size])
        off += size
    return parts


def retention_mixer(q, k, v, g, gn_w):
    b, s, _ = q.shape
    q = rotary(_f32(q).reshape(b, s, RET_HEADS, RET_DK)) * RET_DK ** -0.5
    k = rotary(_f32(k).reshape(b, s, RET_HEADS, RET_DK))
    v = _f32(v).reshape(b, s, RET_HEADS, RET_DV)
    log_gamma = jnp.log(1.0 - 2.0 ** (-5.0 - jnp.arange(RET_HEADS, dtype=jnp.float32)))
    pos = jnp.arange(CHUNK, dtype=jnp.float32)
    intra_decay = jnp.exp(log_gamma[:, None, None] * jnp.abs(pos[:, None] - pos[None, :]))
    q_decay = jnp.exp(log_gamma[:, None] * pos[None, :])[:, :, None]
    k_decay = jnp.exp(log_gamma[:, None] * (CHUNK - pos)[None, :])[:, :, None]
    chunk_decay = jnp.exp(log_gamma * CHUNK)[:, None, None]
    qc, kc, vc = to_chunks(q), to_chunks(k), to_chunks(v)
    scores = jnp.einsum('bhnid,bhnjd->bhnij', qc, kc) * intra_decay[:, None]
    o_intra = jnp.einsum('bhnij,bhnjd->bhnid', scores, vc)

    def step(state, inp):
        qn, kn, vn = inp
        o = jnp.einsum('bhid,bhde->bhie', qn * q_decay, state)
        state = chunk_decay * state + jnp.einsum('bhjd,bhje->bhde', kn * k_decay, vn)
        return state, o

    state0 = jnp.zeros((b, RET_HEADS, RET_DK, RET_DV), jnp.float32)
    _, o_inter = lax.scan(step, state0, (jnp.moveaxis(qc, 2, 0), jnp.moveaxis(kc, 2, 0), jnp.moveaxis(vc, 2, 0)))
    o = from_chunks(o_intra + jnp.moveaxis(o_inter, 0, 2))
    mu = jnp.mean(o, axis=-1, keepdims=True)
    var = jnp.mean(jnp.square(o - mu), axis=-1, keepdims=True)
    o = ((o - mu) * lax.rsqrt(var + EPS)).reshape(b, s, D_MIX) * _f32(gn_w)
    return jax.nn.silu(_f32(g)) * o


def rglru_mixer(xb, gate, conv_w, conv_b, w_r, b_r, w_i, b_i, a_param):
    b, s, _ = xb.shape
    xc = causal_depthwise_conv(_f32(xb), _f32(conv_w), _f32(conv_b))
    xblk = xc.reshape(b, s, LRU_BLOCKS, LRU_BLOCK)
    r = jax.nn.sigmoid(jnp.einsum('bsnk,nkl->bsnl', xblk, _f32(w_r)).reshape(b, s, D_MIX) + _f32(b_r))
    i = jax.nn.sigmoid(jnp.einsum('bsnk,nkl->bsnl', xblk, _f32(w_i)).reshape(b, s, D_MIX) + _f32(b_i))
    log_a = -LRU_C * r * jax.nn.softplus(-_f32(a_param))
    a = jnp.exp(log_a)
    bx = jnp.sqrt(-jnp.expm1(2.0 * log_a)) * (i * xc)

    def combine(c1, c2):
        a1, b1 = c1
        a2, b2 = c2
        return a1 * a2, a2 * b1 + b2

    _, h = lax.associative_scan(combine, (a, bx), axis=1)
    return h * jax.nn.gelu(_f32(gate))


def hgrn2_mixer(q, f_logit, i, g, lb, norm_w):
    b, s, _ = q.shape
    lb = _f32(lb)
    f = lb + (1.0 - lb) * jax.nn.sigmoid(_f32(f_logit))
    log_f = jnp.log(f).reshape(b, s, HG_HEADS, HG_DK)
    k = (1.0 - f).reshape(b, s, HG_HEADS, HG_DK)
    q = jax.nn.silu(_f32(q)).reshape(b, s, HG_HEADS, HG_DK) * HG_DK ** -0.5
    v = _f32(i).reshape(b, s, HG_HEADS, HG_DV)
    causal = jnp.tril(jnp.ones((CHUNK, CHUNK), dtype=bool))

    def step(state, inp):
        qn, kn, vn, lf = inp
        cum = jnp.cumsum(lf, axis=2)
        o_inter = jnp.einsum('bhid,bhde->bhie', qn * jnp.exp(cum), state)
        diff = cum[:, :, :, None, :] - cum[:, :, None, :, :]
        decay = jnp.exp(jnp.where(causal[:, :, None], diff, -jnp.inf))
        attn = jnp.einsum('bhijd,bhjd->bhij', qn[:, :, :, None, :] * decay, kn)
        o = o_inter + jnp.einsum('bhij,bhje->bhie', attn, vn)
        last = cum[:, :, -1:, :]
        state = jnp.exp(last[:, :, 0, :])[..., None] * state + jnp.einsum('bhjd,bhje->bhde', kn * jnp.exp(last - cum), vn)
        return state, o

    state0 = jnp.zeros((b, HG_HEADS, HG_DK, HG_DV), jnp.float32)
    xs = (jnp.moveaxis(to_chunks(q), 2, 0), jnp.moveaxis(to_chunks(k), 2, 0),
          jnp.moveaxis(to_chunks(v), 2, 0), jnp.moveaxis(to_chunks(log_f), 2, 0))
    _, o = lax.scan(step, state0, xs)
    o = from_chunks(jnp.moveaxis(o, 0, 2))
    o = (o * lax.rsqrt(jnp.mean(o * o, axis=-1, keepdims=True) + EPS)).reshape(b, s, D_MIX) * _f32(norm_w)
    return o * jax.nn.silu(_f32(g))


def gated_deltanet_mixer(q, k, v, z, a, beta_logit, conv_w, a_log, dt_bias, norm_w):
    b, s, _ = q.shape
    qkv = jax.nn.silu(causal_depthwise_conv(_f32(jnp.concatenate([q, k, v], axis=-1)), _f32(conv_w)))
    q, k, v = jnp.split(qkv, 3, axis=-1)

    def l2n(t):
        return t * lax.rsqrt(jnp.sum(t * t, axis=-1, keepdims=True) + EPS)

    q = l2n(q.reshape(b, s, DN_HEADS, DN_DK)) * DN_DK ** -0.5
    k = l2n(k.reshape(b, s, DN_HEADS, DN_DK))
    v = v.reshape(b, s, DN_HEADS, DN_DV)
    beta = jax.nn.sigmoid(_f32(beta_logit))
    log_alpha = -jnp.exp(_f32(a_log)) * jax.nn.softplus(_f32(a) + _f32(dt_bias))
    qc, kc, vc = to_chunks(q), to_chunks(k), to_chunks(v)
    betac = to_chunks(beta[..., None])[..., 0]
    g = jnp.cumsum(to_chunks(log_alpha[..., None])[..., 0], axis=-1)
    strict = jnp.tril(jnp.ones((CHUNK, CHUNK), dtype=bool), -1)
    incl = jnp.tril(jnp.ones((CHUNK, CHUNK), dtype=bool))
    decay = jnp.exp(jnp.where(incl, g[..., :, None] - g[..., None, :], -jnp.inf))
    kk = jnp.einsum('bhnid,bhnjd->bhnij', kc, kc)
    lower = jnp.where(strict, betac[..., None] * kk * decay, 0.0)
    eye = jnp.eye(CHUNK, dtype=jnp.float32)
    rhs = jnp.concatenate([vc * betac[..., None], kc * (betac * jnp.exp(g))[..., None]], axis=-1)
    sol = lax.linalg.triangular_solve(eye + lower, rhs, left_side=True, lower=True, unit_diagonal=True)
    u_val, w_key = sol[..., :DN_DV], sol[..., DN_DV:]
    qk = jnp.einsum('bhnid,bhnjd->bhnij', qc, kc) * decay
    q_g = qc * jnp.exp(g)[..., None]
    g_last = g[..., -1:]
    k_tail = kc * jnp.exp(g_last - g)[..., None]
    chunk_decay = jnp.exp(g_last)[..., None]

    def step(state, inp):
        u_n, w_n, qk_n, qg_n, kt_n, cd_n = inp
        v_new = u_n - jnp.einsum('bhcd,bhde->bhce', w_n, state)
        o = jnp.einsum('bhid,bhde->bhie', qg_n, state) + jnp.einsum('bhij,bhje->bhie', qk_n, v_new)
        state = cd_n * state + jnp.einsum('bhjd,bhje->bhde', kt_n, v_new)
        return state, o

    state0 = jnp.zeros((b, DN_HEADS, DN_DK, DN_DV), jnp.float32)
    xs = (jnp.moveaxis(u_val, 2, 0), jnp.moveaxis(w_key, 2, 0), jnp.moveaxis(qk, 2, 0),
          jnp.moveaxis(q_g, 2, 0), jnp.moveaxis(k_tail, 2, 0), jnp.moveaxis(chunk_decay, 2, 0))
    _, o = lax.scan(step, state0, xs)
    o = from_chunks(jnp.moveaxis(o, 0, 2))
    o = o * lax.rsqrt(jnp.mean(o * o, axis=-1, keepdims=True) + EPS) * _f32(norm_w)
    return o.reshape(b, s, D_MIX) * jax.nn.silu(_f32(z))


def setup_inputs(seed: int = 0) -> dict:
    key = jax.random.key(seed)
    ks = jax.random.split(key, 26)
    L = DEPTH

    def nrm(k, shape, scale):
        return jax.random.normal(k, shape, jnp.float32) * scale

    x = nrm(ks[0], (BATCH, SEQ, D_MODEL), 1.0)
    norm_mix = 1.0 + nrm(ks[1], (L, D_MODEL), 0.1)
    w_in = nrm(ks[2], (L, D_MODEL, N_IN), D_MODEL ** -0.5)
    merge_bias = nrm(ks[3], (L, N_BRANCH * D_MODEL), 0.02)
    ret_gn = 1.0 + nrm(ks[4], (L, D_MIX), 0.1)
    lru_conv_w = nrm(ks[5], (L, CONV_WIDTH, D_MIX), CONV_WIDTH ** -0.5)
    lru_conv_b = nrm(ks[6], (L, D_MIX), 0.02)
    lru_w_r = nrm(ks[7], (L, LRU_BLOCKS, LRU_BLOCK, LRU_BLOCK), LRU_BLOCK ** -0.5)
    lru_b_r = nrm(ks[8], (L, D_MIX), 0.02)
    lru_w_i = nrm(ks[9], (L, LRU_BLOCKS, LRU_BLOCK, LRU_BLOCK), LRU_BLOCK ** -0.5)
    lru_b_i = nrm(ks[10], (L, D_MIX), 0.02)
    a_pow = jax.random.uniform(ks[11], (L, D_MIX), dtype=jnp.float32, minval=0.9, maxval=0.999)
    p = a_pow ** (1.0 / LRU_C)
    lru_a = jnp.log(p) - jnp.log1p(-p)
    hg_lb_logits = nrm(ks[12], (L, D_MIX), 0.5)
    hg_norm = 1.0 + nrm(ks[13], (L, D_MIX), 0.1)
    dn_conv_w = nrm(ks[14], (L, CONV_WIDTH, 3 * D_MIX), CONV_WIDTH ** -0.5)
    dn_a_log = jnp.log(jax.random.uniform(ks[15], (L, DN_HEADS), dtype=jnp.float32, minval=1.0, maxval=16.0))
    dt = jnp.exp(jax.random.uniform(ks[16], (L, DN_HEADS), dtype=jnp.float32, minval=math.log(1e-3), maxval=math.log(1e-1)))
    dn_dt_bias = dt + jnp.log(-jnp.expm1(-dt))
    dn_norm = 1.0 + nrm(ks[17], (L, DN_DV), 0.1)
    w_branch = nrm(ks[18], (L, N_BRANCH, D_MIX, D_MODEL), D_MIX ** -0.5)
    w_out = nrm(ks[19], (L, D_MODEL, D_MODEL), D_MODEL ** -0.5)
    norm_ffn = 1.0 + nrm(ks[20], (L, D_MODEL), 0.1)
    w_ffn_gate = nrm(ks[21], (L, D_MODEL, FFN_HIDDEN), D_MODEL ** -0.5)
    w_ffn_up = nrm(ks[22], (L, D_MODEL, FFN_HIDDEN), D_MODEL ** -0.5)
    w_ffn_down = nrm(ks[23], (L, FFN_HIDDEN, D_MODEL), FFN_HIDDEN ** -0.5)
    norm_final = 1.0 + nrm(ks[24], (D_MODEL,), 0.1)
    return {'x': x, 'norm_mix': norm_mix, 'w_in': w_in, 'merge_bias': merge_bias, 'ret_gn': ret_gn,
            'lru_conv_w': lru_conv_w, 'lru_conv_b': lru_conv_b, 'lru_w_r': lru_w_r, 'lru_b_r': lru_b_r,
            'lru_w_i': lru_w_i, 'lru_b_i': lru_b_i, 'lru_a': lru_a, 'hg_lb_logits': hg_lb_logits,
            'hg_norm': hg_norm, 'dn_conv_w': dn_conv_w, 'dn_a_log': dn_a_log, 'dn_dt_bias': dn_dt_bias,
            'dn_norm': dn_norm, 'w_branch': w_branch, 'w_out': w_out, 'norm_ffn': norm_ffn,
            'w_ffn_gate': w_ffn_gate, 'w_ffn_up': w_ffn_up, 'w_ffn_down': w_ffn_down, 'norm_final': norm_final}


def reference(x, norm_mix, w_in, merge_bias, ret_gn, lru_conv_w, lru_conv_b, lru_w_r, lru_b_r, lru_w_i,
              lru_b_i, lru_a, hg_lb_logits, hg_norm, dn_conv_w, dn_a_log, dn_dt_bias, dn_norm, w_branch,
              w_out, norm_ffn, w_ffn_gate, w_ffn_up, w_ffn_down, norm_final):
    b, s, _ = x.shape
    lb_all = jnp.cumsum(jax.nn.softmax(_f32(hg_lb_logits), axis=0), axis=0)
    lb_all = lb_all - lb_all[0:1]
    for l in range(DEPTH):
        h = rmsnorm(x, norm_mix[l])
        u = h @ w_in[l]
        (r_q, r_k, r_v, r_g, l_x, l_gate, h_q, h_f, h_i, h_g,
         d_q, d_k, d_v, d_z, d_a, d_b, gate_logits) = split_columns(u)
        y_ret = retention_mixer(r_q, r_k, r_v, r_g, ret_gn[l])
        y_lru = rglru_mixer(l_x, l_gate, lru_conv_w[l], lru_conv_b[l], lru_w_r[l], lru_b_r[l],
                            lru_w_i[l], lru_b_i[l], lru_a[l])
        y_hg = hgrn2_mixer(h_q, h_f, h_i, h_g, lb_all[l], hg_norm[l])
        y_dn = gated_deltanet_mixer(d_q, d_k, d_v, d_z, d_a, d_b, dn_conv_w[l], dn_a_log[l],
                                    dn_dt_bias[l], dn_norm[l])
        gates = jax.nn.sigmoid(_f32(gate_logits) + _f32(merge_bias[l])).reshape(b, s, N_BRANCH, D_MODEL)
        branches = (y_ret, y_lru, y_hg, y_dn)
        merged = jnp.zeros((b, s, D_MODEL), jnp.float32)
        for n in range(N_BRANCH):
            merged = merged + gates[:, :, n, :] * _f32(branches[n].astype(x.dtype) @ w_branch[l, n])
        x = x + merged.astype(x.dtype) @ w_out[l]
        h2 = rmsnorm(x, norm_ffn[l])
        x = x + (jax.nn.silu(h2 @ w_ffn_gate[l]) * (h2 @ w_ffn_up[l])) @ w_ffn_down[l]
    return rmsnorm(x, norm_final)
```

```python
import math
import numpy as np
from contextlib import ExitStack
import concourse.bass as bass
import concourse.mybir as mybir
from concourse.bass_utils import run_bass_kernel_spmd

F32 = mybir.dt.float32
BF16 = mybir.dt.bfloat16
AF = mybir.ActivationFunctionType
ALU = mybir.AluOpType
AX = mybir.AxisListType


class Sch:
    NDQ = 6

    def __init__(self, nc, es, same_eng_sync=True):
        self.nc = nc
        self.E = {'pe': nc.tensor, 'dve': nc.vector, 'act': nc.scalar,
                  'pool': nc.gpsimd, 'sp': nc.sync}
        self.sem = {}
        self.cnt = {}
        for e in ['pe', 'dve', 'act', 'pool']:
            self.sem[e] = es.enter_context(nc.semaphore("s_" + e))
            self.cnt[e] = 0
        self.dq = {}
        self.dqi = {}
        for q in ['sp', 'pool', 'act']:
            names = []
            for i in range(self.NDQ):
                n = "d_%s%d" % (q, i)
                self.sem[n] = es.enter_context(nc.semaphore(n))
                self.cnt[n] = 0
                names.append(n)
            self.dq[q] = names
            self.dqi[q] = 0
        self.seen = {e: {} for e in self.E}
        self.lastw = {}
        self.readers = {}
        self.pend = {e: ([], []) for e in self.E}
        self.same = same_eng_sync
        self.nwaits = 0
        self.nops = 0

    def _need(self, eng, reads, writes):
        need = {}

        def add(w):
            if w is None:
                return
            s, v = w
            if v is None:
                if s == eng:
                    return
                raise RuntimeError("dependency on un-flushed deferred op")
            if need.get(s, 0) < v:
                need[s] = v
        for k in reads:
            add(self.lastw.get(k))
        for k in writes:
            add(self.lastw.get(k))
            for s, v in self.readers.get(k, {}).items():
                add((s, v))
        return need

    def _waits(self, eng, need):
        e = self.E[eng]
        for s, v in need.items():
            if s == eng and (eng == 'pe' or not self.same):
                continue
            if self.seen[eng].get(s, 0) >= v:
                continue
            e.wait_ge(self.sem[s], v)
            self.seen[eng][s] = v
            self.nwaits += 1

    def _record(self, semname, val, reads, writes):
        for k in reads:
            self.readers.setdefault(k, {})[semname] = val
        for k in writes:
            self.lastw[k] = (semname, val)
            self.readers[k] = {}

    def op(self, eng, fn, reads=(), writes=(), inc=True):
        self.nops += 1
        need = self._need(eng, reads, writes)
        self._waits(eng, need)
        ins = fn(self.E[eng])
        pr, pw = self.pend[eng]
        if inc:
            self.cnt[eng] += 1
            ins.then_inc(self.sem[eng], 1)
            self._record(eng, self.cnt[eng], list(reads) + pr, list(writes) + pw)
            self.pend[eng] = ([], [])
        else:
            for k in reads:
                self.readers.setdefault(k, {})[eng] = None
            for k in writes:
                self.lastw[k] = (eng, None)
                self.readers[k] = {}
            pr.extend(reads)
            pw.extend(writes)
        return ins

    def dma(self, q, out, in_, reads=(), writes=()):
        self.nops += 1
        names = self.dq[q]
        s = names[self.dqi[q] % len(names)]
        self.dqi[q] += 1
        need = self._need(q, reads, writes)
        if self.cnt[s] > 0:
            need[s] = max(need.get(s, 0), self.cnt[s])
        self._waits(q, need)
        self.cnt[s] += 16
        ins = self.E[q].dma_start(out=out, in_=in_)
        ins.then_inc(self.sem[s], 16)
        self._record(s, self.cnt[s], reads, writes)
        return ins

    def finish(self, eng='sp'):
        need = {}
        for q, names in self.dq.items():
            for s in names:
                if self.cnt[s] > 0:
                    need[s] = self.cnt[s]
        for e in ['pe', 'dve', 'act', 'pool']:
            if self.cnt[e] > 0:
                need[e] = self.cnt[e]
        self._waits(eng, need)


def _barrier(self):
    need = {}
    for s, c in self.cnt.items():
        if c > 0:
            need[s] = c
    for e in ['pe', 'dve', 'act', 'pool', 'sp']:
        n2 = {s: v for s, v in need.items() if s != e}
        self._waits(e, n2)


Sch.barrier = _barrier


GC = 2.0 * math.sqrt(2.0 / math.pi)


def emit_lru(S, nc, es, T, io, TT=512):
    A = lambda name, shape, dt=F32: es.enter_context(nc.sbuf_tensor("lru_" + name, shape, dt))
    pp = A("pp", [128, 16]); wr = A("wr", [128, 128]); wi = A("wi", [128, 128])
    cv = A("cv", [128, 1]); h0 = A("h0", [128, 1])
    xb = [A("xb%d" % i, [128, TT + 3]) for i in range(2)]
    gt = [A("gt%d" % i, [128, TT]) for i in range(2)]
    xc = A("xc", [128, TT]); r = A("r", [128, TT]); ii = A("ii", [128, TT]); a = A("a", [128, TT])
    t1 = A("t1", [128, TT]); t2 = A("t2", [128, TT]); hh = [A("hh%d" % i, [128, TT]) for i in range(2)]
    yo = [A("yo%d" % i, [128, TT]) for i in range(2)]
    psr = es.enter_context(nc.psum_tensor("lru_psr", [128, TT], F32))
    psi = es.enter_context(nc.psum_tensor("lru_psi", [128, TT], F32))
    K = lambda *a_: ("lru",) + a_
    S.dma('sp', pp[:], io['pp'], writes=[K("pp")])
    S.dma('sp', wr[:], io['wr'], writes=[K("wr")])
    S.dma('sp', wi[:], io['wi'], writes=[K("wi")])
    S.op('act', lambda e: e.activation(out=cv[:], in_=pp[:, 7:8], func=AF.Exp, scale=-1.0), reads=[K("pp")], writes=[K("cv")])
    S.op('act', lambda e: e.activation(out=cv[:], in_=cv[:], func=AF.Ln, bias=1.0), reads=[K("cv")], writes=[K("cv")])
    S.op('dve', lambda e: e.tensor_scalar(out=cv[:], in0=cv[:], scalar1=-8.0, scalar2=None, op0=ALU.mult), reads=[K("cv")], writes=[K("cv")])
    S.op('dve', lambda e: e.memset(h0[:], 0.0), writes=[K("h0")])
    S.op('dve', lambda e: e.memset(xb[1][:, TT:TT + 3], 0.0), writes=[K("xbm", 1)])
    nt = T // TT
    for ti in range(nt):
        b = ti % 2
        pb = (ti + 1) % 2
        c0 = ti * TT
        S.dma('sp', xb[b][:, 3:TT + 3], io['lx'][:, c0:c0 + TT], writes=[K("xbm", b)])
        S.dma('sp', gt[b][:], io['lg'][:, c0:c0 + TT], writes=[K("gt", b)])
        S.op('pool', lambda e: e.tensor_copy(out=xb[b][:, 0:3], in_=xb[pb][:, TT:TT + 3]), reads=[K("xbm", pb), K("xb", pb)], writes=[K("xb", b)])
        X = [K("xb", b), K("xbm", b)]
        S.op('dve', lambda e: e.tensor_scalar(out=xc[:], in0=xb[b][:, 0:TT], scalar1=pp[:, 0:1], scalar2=pp[:, 4:5], op0=ALU.mult, op1=ALU.add), reads=X + [K("pp")], writes=[K("xc")])
        for j in (1, 2, 3):
            S.op('dve', lambda e: e.scalar_tensor_tensor(out=xc[:], in0=xb[b][:, j:j + TT], scalar=pp[:, j:j + 1], in1=xc[:], op0=ALU.mult, op1=ALU.add), reads=X + [K("pp"), K("xc")], writes=[K("xc")])
        S.op('pe', lambda e: e.matmul(psr[:], lhsT=wr[:], rhs=xc[:], start=True, stop=True), reads=[K("wr"), K("xc")], writes=[K("psr")])
        S.op('pe', lambda e: e.matmul(psi[:], lhsT=wi[:], rhs=xc[:], start=True, stop=True), reads=[K("wi"), K("xc")], writes=[K("psi")])
        S.op('act', lambda e: e.activation(out=r[:], in_=psr[:], func=AF.Sigmoid, bias=pp[:, 5:6]), reads=[K("psr"), K("pp")], writes=[K("r")])
        S.op('act', lambda e: e.activation(out=ii[:], in_=psi[:], func=AF.Sigmoid, bias=pp[:, 6:7]), reads=[K("psi"), K("pp")], writes=[K("ii")])
        S.op('act', lambda e: e.activation(out=a[:], in_=r[:], func=AF.Exp, scale=cv[:, 0:1]), reads=[K("r"), K("cv")], writes=[K("a")])
        S.op('dve', lambda e: e.tensor_tensor(out=t1[:], in0=a[:], in1=a[:], op=ALU.mult), reads=[K("a")], writes=[K("t1")])
        S.op('dve', lambda e: e.tensor_scalar(out=t1[:], in0=t1[:], scalar1=-1.0, scalar2=1.0, op0=ALU.mult, op1=ALU.add), reads=[K("t1")], writes=[K("t1")])
        S.op('act', lambda e: e.activation(out=t1[:], in_=t1[:], func=AF.Sqrt), reads=[K("t1")], writes=[K("t1")])
        S.op('dve', lambda e: e.tensor_tensor(out=t2[:], in0=ii[:], in1=xc[:], op=ALU.mult), reads=[K("ii"), K("xc")], writes=[K("t2")])
        S.op('dve', lambda e: e.tensor_tensor(out=t2[:], in0=t2[:], in1=t1[:], op=ALU.mult), reads=[K("t1"), K("t2")], writes=[K("t2")])
        init = h0[:, 0:1] if ti == 0 else hh[pb][:, TT - 1:TT]
        S.op('dve', lambda e: e.tensor_tensor_scan(out=hh[b][:], data0=a[:], data1=t2[:], initial=init, op0=ALU.mult, op1=ALU.add), reads=[K("a"), K("t2"), K("h0"), K("hh", pb)], writes=[K("hh", b)])
        S.op('pool', lambda e: e.tensor_tensor(out=t1[:], in0=gt[b][:], in1=gt[b][:], op=ALU.mult), reads=[K("gt", b)], writes=[K("t1")])
        S.op('pool', lambda e: e.tensor_scalar(out=t1[:], in0=t1[:], scalar1=0.044715, scalar2=1.0, op0=ALU.mult, op1=ALU.add), reads=[K("t1")], writes=[K("t1")])
        S.op('pool', lambda e: e.tensor_tensor(out=t1[:], in0=t1[:], in1=gt[b][:], op=ALU.mult), reads=[K("t1"), K("gt", b)], writes=[K("t1")])
        S.op('act', lambda e: e.activation(out=t1[:], in_=t1[:], func=AF.Sigmoid, scale=GC), reads=[K("t1")], writes=[K("t1")])
        S.op('pool', lambda e: e.tensor_tensor(out=t1[:], in0=t1[:], in1=gt[b][:], op=ALU.mult), reads=[K("t1"), K("gt", b)], writes=[K("t1")])
        S.op('dve', lambda e: e.tensor_tensor(out=yo[b][:], in0=hh[b][:], in1=t1[:], op=ALU.mult), reads=[K("hh", b), K("t1")], writes=[K("yo", b)])
        S.dma('sp', io['y'][:, c0:c0 + TT], yo[b][:], reads=[K("yo", b)])


EPS = 1e-6


def hg_consts():
    f = np.arange(512)
    p = np.arange(64)[:, None]
    maskc = ((f[None, :] % 64) >= p).astype(np.float32)
    reset = np.ones((128, 512), np.float32); reset[:, ::64] = 0.0
    return dict(ident=np.eye(128, dtype=np.float32), maskc=maskc, reset=reset, ones=np.ones((128, 128), np.float32))


def emit_hg(S, nc, es, T, io, TT=512):
    A = lambda name, shape, dt=F32: es.enter_context(nc.sbuf_tensor("hg_" + name, shape, dt))
    P = lambda name, shape: es.enter_context(nc.psum_tensor("hg_" + name, shape, F32))
    K = lambda *a_: ("hg",) + a_
    NCH = TT // 64
    pp = A("pp", [128, 8]); ident = A("ident", [128, 128]); ones = A("ones", [128, 128])
    maskc = A("maskc", [64, TT]); reset = A("reset", [128, TT])
    lb = A("lb", [128, 1]); oml = A("oml", [128, 1]); tmpc = A("tmpc", [128, 2])
    qin = [A("qin%d" % i, [128, TT]) for i in range(2)]
    fin = [A("fin%d" % i, [128, TT]) for i in range(2)]
    gin = [A("gin%d" % i, [128, TT]) for i in range(2)]
    vin = [A("vin%d" % i, [64, NCH, 128]) for i in range(2)]
    fv = A("fv", [128, TT]); lf = A("lf", [128, TT]); kk = A("kk", [128, TT]); cum = A("cum", [128, TT])
    dq = A("dq", [128, TT]); eq = A("eq", [128, TT]); qs = A("qs", [128, TT])
    qt = A("qt", [128, TT]); kt = A("kt", [128, TT]); qi = A("qi", [128, TT]); kh = A("kh", [128, TT])
    elast = A("elast", [128, NCH]); khT = A("khT", [64, NCH, 128]); attT = A("attT", [64, TT])
    Sst = [A("S%d" % i, [128, 128]) for i in range(2)]
    sq = A("sq", [128, TT]); rs = A("rs", [128, TT]); yo = [A("yo%d" % i, [128, TT]) for i in range(2)]
    ps_tr = P("ps_tr", [128, NCH * 128]); ps_att = P("ps_att", [128, TT]); ps_o = P("ps_o", [128, TT])
    ps_S = P("ps_S", [128, 512]); ps_ss = P("ps_ss", [128, TT])
    for nm, t_ in (("pp", pp), ("ident", ident), ("ones", ones), ("maskc", maskc), ("reset", reset)):
        S.dma('sp', t_[:], io[nm], writes=[K(nm)])
    S.op('act', lambda e: e.activation(out=tmpc[:], in_=pp[:, 0:2], func=AF.Exp), reads=[K("pp")], writes=[K("tmpc")])
    S.op('dve', lambda e: e.tensor_tensor(out=lb[:], in0=tmpc[:, 0:1], in1=tmpc[:, 1:2], op=ALU.add), reads=[K("tmpc")], writes=[K("lb")])
    S.op('dve', lambda e: e.reciprocal(out=lb[:], in_=lb[:]), reads=[K("lb")], writes=[K("lb")])
    S.op('dve', lambda e: e.tensor_tensor(out=lb[:], in0=lb[:], in1=tmpc[:, 1:2], op=ALU.mult), reads=[K("lb"), K("tmpc")], writes=[K("lb")])
    S.op('dve', lambda e: e.tensor_tensor(out=lb[:], in0=lb[:], in1=pp[:, 2:3], op=ALU.mult), reads=[K("lb"), K("pp")], writes=[K("lb")])
    S.op('dve', lambda e: e.tensor_scalar(out=oml[:], in0=lb[:], scalar1=-1.0, scalar2=1.0, op0=ALU.mult, op1=ALU.add), reads=[K("lb")], writes=[K("oml")])
    S.op('dve', lambda e: e.memset(Sst[0][:], 0.0), writes=[K("S", 0)])
    scale = 128.0 ** -0.5
    nt = T // TT
    g = 0
    for ti in range(nt):
        b = ti % 2
        c0 = ti * TT
        S.dma('sp', qin[b][:], io['q'][:, c0:c0 + TT], writes=[K("qin", b)])
        S.dma('sp', fin[b][:], io['f'][:, c0:c0 + TT], writes=[K("fin", b)])
        S.dma('sp', gin[b][:], io['g'][:, c0:c0 + TT], writes=[K("gin", b)])
        S.dma('sp', vin[b][:], io['v'][:, ti * NCH:(ti + 1) * NCH, :], writes=[K("vin", b)])
        S.op('act', lambda e: e.activation(out=fv[:], in_=fin[b][:], func=AF.Sigmoid), reads=[K("fin", b)], writes=[K("fv")])
        S.op('dve', lambda e: e.tensor_scalar(out=fv[:], in0=fv[:], scalar1=oml[:, 0:1], scalar2=lb[:, 0:1], op0=ALU.mult, op1=ALU.add), reads=[K("fv"), K("oml"), K("lb")], writes=[K("fv")])
        S.op('act', lambda e: e.activation(out=lf[:], in_=fv[:], func=AF.Ln), reads=[K("fv")], writes=[K("lf")])
        S.op('pool', lambda e: e.tensor_scalar(out=kk[:], in0=fv[:], scalar1=-1.0, scalar2=1.0, op0=ALU.mult, op1=ALU.add), reads=[K("fv")], writes=[K("kk")])
        S.op('dve', lambda e: e.tensor_tensor_scan(out=cum[:], data0=reset[:], data1=lf[:], initial=0.0, op0=ALU.mult, op1=ALU.add), reads=[K("reset"), K("lf")], writes=[K("cum")])
        cum3 = cum[:].rearrange("p (c j) -> p c j", j=64)
        v3 = lambda t_: t_[:].rearrange("p (c j) -> p c j", j=64)
        S.op('dve', lambda e: e.tensor_tensor(out=v3(dq), in0=cum3, in1=cum3[:, :, 31:32].to_broadcast([128, NCH, 64]), op=ALU.subtract), reads=[K("cum")], writes=[K("dq")])
        S.op('act', lambda e: e.activation(out=eq[:], in_=dq[:], func=AF.Exp), reads=[K("dq")], writes=[K("eq")])
        S.op('act', lambda e: e.activation(out=qs[:], in_=qin[b][:], func=AF.Silu), reads=[K("qin", b)], writes=[K("qs")])
        S.op('dve', lambda e: e.scalar_tensor_tensor(out=qt[:], in0=qs[:], scalar=scale, in1=eq[:], op0=ALU.mult, op1=ALU.mult), reads=[K("qs"), K("eq")], writes=[K("qt")])
        S.op('act', lambda e: e.activation(out=eq[:], in_=dq[:], func=AF.Exp, scale=-1.0), reads=[K("dq"), K("qt")], writes=[K("eq")])
        S.op('dve', lambda e: e.tensor_tensor(out=kt[:], in0=kk[:], in1=eq[:], op=ALU.mult), reads=[K("kk"), K("eq")], writes=[K("kt")])
        S.op('act', lambda e: e.activation(out=eq[:], in_=cum[:], func=AF.Exp), reads=[K("cum"), K("kt")], writes=[K("eq")])
        S.op('dve', lambda e: e.scalar_tensor_tensor(out=qi[:], in0=qs[:], scalar=scale, in1=eq[:], op0=ALU.mult, op1=ALU.mult), reads=[K("qs"), K("eq")], writes=[K("qi")])
        S.op('act', lambda e: e.activation(out=elast[:], in_=cum3[:, :, 63], func=AF.Exp), reads=[K("cum")], writes=[K("elast")])
        S.op('dve', lambda e: e.tensor_tensor(out=v3(dq), in0=cum3, in1=cum3[:, :, 63:64].to_broadcast([128, NCH, 64]), op=ALU.subtract), reads=[K("cum"), K("dq")], writes=[K("dq")])
        S.op('act', lambda e: e.activation(out=eq[:], in_=dq[:], func=AF.Exp, scale=-1.0), reads=[K("dq"), K("qi")], writes=[K("eq")])
        S.op('dve', lambda e: e.tensor_tensor(out=kh[:], in0=kk[:], in1=eq[:], op=ALU.mult), reads=[K("kk"), K("eq")], writes=[K("kh")])
        for n in range(NCH):
            S.op('pe', lambda e: e.transpose(out=ps_tr[0:64, n * 128:(n + 1) * 128], in_=kh[:, n * 64:(n + 1) * 64], identity=ident[:]),
                 reads=[K("kh"), K("ident")], writes=[K("ps_tr")], inc=(n == NCH - 1))
        S.op('act', lambda e: e.copy(out=khT[:].rearrange("p c d -> p (c d)"), in_=ps_tr[0:64, :]), reads=[K("ps_tr")], writes=[K("khT")])
        for n in range(NCH):
            S.op('pe', lambda e: e.matmul(ps_att[0:64, n * 64:(n + 1) * 64], lhsT=kt[:, n * 64:(n + 1) * 64], rhs=qt[:, n * 64:(n + 1) * 64], start=True, stop=True),
                 reads=[K("kt"), K("qt")], writes=[K("ps_att")], inc=(n == NCH - 1))
        S.op('dve', lambda e: e.tensor_tensor(out=attT[:], in0=ps_att[0:64, :], in1=maskc[:], op=ALU.mult), reads=[K("ps_att"), K("maskc")], writes=[K("attT")])
        for n in range(NCH):
            cs = slice(n * 64, (n + 1) * 64)
            sc, sn = g % 2, (g + 1) % 2
            S.op('pe', lambda e: e.matmul(ps_o[:, cs], lhsT=vin[b][:, n, :], rhs=attT[:, cs], start=True, stop=False),
                 reads=[K("vin", b), K("attT")], writes=[K("ps_o")], inc=False)
            S.op('pe', lambda e: e.matmul(ps_o[:, cs], lhsT=Sst[sc][:], rhs=qi[:, cs], start=False, stop=True),
                 reads=[K("S", sc), K("qi")], writes=[K("ps_o")], inc=False)
            S.op('pe', lambda e: e.matmul(ps_S[:, 0:128], lhsT=khT[:, n, :], rhs=vin[b][:, n, :], start=True, stop=True),
                 reads=[K("khT"), K("vin", b)], writes=[K("ps_S")])
            S.op('dve', lambda e: e.scalar_tensor_tensor(out=Sst[sn][:], in0=Sst[sc][:], scalar=elast[:, n:n + 1], in1=ps_S[:, 0:128], op0=ALU.mult, op1=ALU.add),
                 reads=[K("S", sc), K("elast"), K("ps_S")], writes=[K("S", sn)])
            g += 1
        S.op('act', lambda e: e.activation(out=sq[:], in_=ps_o[:], func=AF.Square), reads=[K("ps_o")], writes=[K("sq")])
        S.op('pe', lambda e: e.matmul(ps_ss[:], lhsT=ones[:], rhs=sq[:], start=True, stop=True), reads=[K("ones"), K("sq")], writes=[K("ps_ss")])
        S.op('dve', lambda e: e.tensor_scalar(out=rs[:], in0=ps_ss[:], scalar1=1.0 / 128.0, scalar2=EPS, op0=ALU.mult, op1=ALU.add), reads=[K("ps_ss")], writes=[K("rs")])
        S.op('act', lambda e: e.activation(out=rs[:], in_=rs[:], func=AF.Sqrt), reads=[K("rs")], writes=[K("rs")])
        S.op('dve', lambda e: e.reciprocal(out=rs[:], in_=rs[:]), reads=[K("rs")], writes=[K("rs")])
        S.op('dve', lambda e: e.scalar_tensor_tensor(out=rs[:], in0=ps_o[:], scalar=pp[:, 3:4], in1=rs[:], op0=ALU.mult, op1=ALU.mult), reads=[K("ps_o"), K("pp"), K("rs")], writes=[K("rs")])
        S.op('act', lambda e: e.activation(out=sq[:], in_=gin[b][:], func=AF.Silu), reads=[K("gin", b), K("sq")], writes=[K("sq")])
        S.op('dve', lambda e: e.tensor_tensor(out=yo[b][:], in0=rs[:], in1=sq[:], op=ALU.mult), reads=[K("rs"), K("sq")], writes=[K("yo", b)])
        S.dma('sp', io['y'][:, c0:c0 + TT], yo[b][:], reads=[K("yo", b)])


EPS = 1e-6


def ret_consts(head, T):
    gamma = 1.0 - 2.0 ** (-5.0 - head)
    lg = np.log(np.float32(gamma)).astype(np.float32)
    scale = 256.0 ** -0.5
    f = np.arange(512) % 64
    qdec = np.tile((np.exp(lg * f) * scale)[None, :], (128, 1)).astype(np.float32)
    kdec = np.tile(np.exp(lg * (64 - f))[None, :], (128, 1)).astype(np.float32)
    p = np.arange(64)[:, None]
    dmask = (np.exp(lg * np.abs(f[None, :] - p)) * scale).astype(np.float32)
    half = 128
    inv_freq = 1.0 / (10000.0 ** (np.arange(half, dtype=np.float32) / half))
    ang = np.arange(T, dtype=np.float32)[None, :] * inv_freq[:, None].astype(np.float32)
    g64 = np.full((128,), np.exp(lg * 64), np.float32)
    return dict(qdec=qdec, kdec=kdec, dmask=dmask, cos=np.cos(ang).astype(np.float32), sin=np.sin(ang).astype(np.float32),
                ident=np.eye(128, dtype=np.float32), ones=np.ones((128, 128), np.float32)), g64


def emit_ret(S, nc, es, T, io, TT=512):
    A = lambda name, shape, dt=F32: es.enter_context(nc.sbuf_tensor("rt_" + name, shape, dt))
    P = lambda name, shape: es.enter_context(nc.psum_tensor("rt_" + name, shape, F32))
    K = lambda *a_: ("rt",) + a_
    NCH = TT // 64
    pp = A("pp", [128, 4]); ident = A("ident", [128, 128]); ones = A("ones", [128, 128])
    qdec = A("qdec", [128, TT]); kdec = A("kdec", [128, TT]); dmask = A("dmask", [64, TT])
    qin = [A("qin%d" % i, [128, 2, TT]) for i in range(2)]
    kin = [A("kin%d" % i, [128, 2, TT]) for i in range(2)]
    gin = [A("gin%d" % i, [128, 2, TT]) for i in range(2)]
    vin = [A("vin%d" % i, [64, NCH, 256]) for i in range(2)]
    cs_ = [A("cos%d" % i, [128, TT]) for i in range(2)]
    sn_ = [A("sin%d" % i, [128, TT]) for i in range(2)]
    ta = A("ta", [128, TT]); tb = A("tb", [128, TT])
    qr = A("qr", [128, 2, TT]); kr = A("kr", [128, 2, TT]); qi = A("qi", [128, 2, TT]); kh = A("kh", [128, 2, TT])
    khT = A("khT", [64, NCH, 256]); attT = A("attT", [64, TT])
    Sst = [A("S%d" % i, [128, 2, 256]) for i in range(2)]
    osb = A("osb", [128, 2, TT]); sq = A("sq", [128, 2, TT]); mean = A("mean", [128, TT]); rs = A("rs", [128, TT])
    yo = [A("yo%d" % i, [128, 2, TT]) for i in range(2)]
    ps_tr = P("ps_tr", [128, 1024]); ps_att = P("ps_att", [128, TT]); ps_o = [P("ps_o%d" % i, [128, TT]) for i in range(2)]
    ps_S = P("ps_S", [128, 512]); ps_sum = P("ps_sum", [128, TT]); ps_sq = P("ps_sq", [128, TT])
    for nm, t_ in (("pp", pp), ("ident", ident), ("ones", ones), ("qdec", qdec), ("kdec", kdec), ("dmask", dmask)):
        S.dma('sp', t_[:], io[nm], writes=[K(nm)])
    S.op('dve', lambda e: e.memset(Sst[0][:], 0.0), writes=[K("S", 0)])
    nt = T // TT
    g = 0
    for ti in range(nt):
        b = ti % 2
        c0 = ti * TT
        S.dma('sp', qin[b][:], io['q'][:, :, c0:c0 + TT], writes=[K("qin", b)])
        S.dma('sp', kin[b][:], io['k'][:, :, c0:c0 + TT], writes=[K("kin", b)])
        S.dma('sp', gin[b][:], io['g'][:, :, c0:c0 + TT], writes=[K("gin", b)])
        S.dma('sp', vin[b][:], io['v'][:, ti * NCH:(ti + 1) * NCH, :], writes=[K("vin", b)])
        S.dma('sp', cs_[b][:], io['cos'][:, c0:c0 + TT], writes=[K("cos", b)])
        S.dma('sp', sn_[b][:], io['sin'][:, c0:c0 + TT], writes=[K("sin", b)])
        for (src, skey, dst, dkey) in ((qin[b], K("qin", b), qr, K("qr")), (kin[b], K("kin", b), kr, K("kr"))):
            C_, S_ = [K("cos", b)], [K("sin", b)]
            S.op('dve', lambda e: e.tensor_tensor(out=ta[:], in0=src[:, 0, :], in1=cs_[b][:], op=ALU.mult), reads=[skey] + C_, writes=[K("ta")])
            S.op('pool', lambda e: e.tensor_tensor(out=tb[:], in0=src[:, 1, :], in1=sn_[b][:], op=ALU.mult), reads=[skey] + S_, writes=[K("tb")])
            S.op('dve', lambda e: e.tensor_tensor(out=dst[:, 0, :], in0=ta[:], in1=tb[:], op=ALU.subtract), reads=[K("ta"), K("tb")], writes=[dkey])
            S.op('dve', lambda e: e.tensor_tensor(out=ta[:], in0=src[:, 0, :], in1=sn_[b][:], op=ALU.mult), reads=[skey] + S_, writes=[K("ta")])
            S.op('pool', lambda e: e.tensor_tensor(out=tb[:], in0=src[:, 1, :], in1=cs_[b][:], op=ALU.mult), reads=[skey] + C_, writes=[K("tb")])
            S.op('dve', lambda e: e.tensor_tensor(out=dst[:, 1, :], in0=ta[:], in1=tb[:], op=ALU.add), reads=[K("ta"), K("tb")], writes=[dkey])
        S.op('pool', lambda e: e.tensor_tensor(out=qi[:], in0=qr[:], in1=qdec[:].unsqueeze(1).to_broadcast([128, 2, TT]), op=ALU.mult), reads=[K("qr"), K("qdec")], writes=[K("qi")])
        S.op('dve', lambda e: e.tensor_tensor(out=kh[:], in0=kr[:], in1=kdec[:].unsqueeze(1).to_broadcast([128, 2, TT]), op=ALU.mult), reads=[K("kr"), K("kdec")], writes=[K("kh")])
        for hf in range(2):
            for n4 in range(4):
                n = hf * 4 + n4
                for dt in range(2):
                    S.op('pe', lambda e: e.transpose(out=ps_tr[0:64, n4 * 256 + dt * 128:n4 * 256 + (dt + 1) * 128], in_=kh[:, dt, n * 64:(n + 1) * 64], identity=ident[:]),
                         reads=[K("kh"), K("ident")], writes=[K("ps_tr")], inc=(n4 == 3 and dt == 1))
            S.op('act', lambda e: e.copy(out=khT[:, hf * 4:(hf + 1) * 4, :].rearrange("p c d -> p (c d)"), in_=ps_tr[0:64, :]), reads=[K("ps_tr")], writes=[K("khT")])
        for n in range(NCH):
            cs = slice(n * 64, (n + 1) * 64)
            S.op('pe', lambda e: e.matmul(ps_att[0:64, cs], lhsT=kr[:, 0, cs], rhs=qr[:, 0, cs], start=True, stop=False), reads=[K("kr"), K("qr")], writes=[K("ps_att")], inc=False)
            S.op('pe', lambda e: e.matmul(ps_att[0:64, cs], lhsT=kr[:, 1, cs], rhs=qr[:, 1, cs], start=False, stop=True), reads=[K("kr"), K("qr")], writes=[K("ps_att")], inc=(n == NCH - 1))
        S.op('dve', lambda e: e.tensor_tensor(out=attT[:], in0=ps_att[0:64, :], in1=dmask[:], op=ALU.mult), reads=[K("ps_att"), K("dmask")], writes=[K("attT")])
        for n in range(NCH):
            cs = slice(n * 64, (n + 1) * 64)
            sc, sn = g % 2, (g + 1) % 2
            for et in range(2):
                es_ = slice(et * 128, (et + 1) * 128)
                S.op('pe', lambda e: e.matmul(ps_o[et][:, cs], lhsT=vin[b][:, n, es_], rhs=attT[:, cs], start=True, stop=False), reads=[K("vin", b), K("attT")], writes=[K("ps_o", et)], inc=False)
                S.op('pe', lambda e: e.matmul(ps_o[et][:, cs], lhsT=Sst[sc][:, 0, es_], rhs=qi[:, 0, cs], start=False, stop=False), reads=[K("S", sc), K("qi")], writes=[K("ps_o", et)], inc=False)
                S.op('pe', lambda e: e.matmul(ps_o[et][:, cs], lhsT=Sst[sc][:, 1, es_], rhs=qi[:, 1, cs], start=False, stop=True), reads=[K("S", sc), K("qi")], writes=[K("ps_o", et)], inc=False)
            for dt in range(2):
                S.op('pe', lambda e: e.matmul(ps_S[:, dt * 256:(dt + 1) * 256], lhsT=khT[:, n, dt * 128:(dt + 1) * 128], rhs=vin[b][:, n, :], start=True, stop=True),
                     reads=[K("khT"), K("vin", b)], writes=[K("ps_S")], inc=(dt == 1))
            S.op('dve', lambda e: e.scalar_tensor_tensor(out=Sst[sn][:].rearrange("p a b -> p (a b)"), in0=Sst[sc][:].rearrange("p a b -> p (a b)"), scalar=pp[:, 0:1], in1=ps_S[:], op0=ALU.mult, op1=ALU.add),
                 reads=[K("S", sc), K("pp"), K("ps_S")], writes=[K("S", sn)])
            g += 1
        for et in range(2):
            S.op('act', lambda e: e.copy(out=osb[:, et, :], in_=ps_o[et][:]), reads=[K("ps_o", et)], writes=[K("osb", et)])
            S.op('act', lambda e: e.activation(out=sq[:, et, :], in_=ps_o[et][:], func=AF.Square), reads=[K("ps_o", et)], writes=[K("sq", et)])
        for et in range(2):
            S.op('pe', lambda e: e.matmul(ps_sum[:], lhsT=ones[:], rhs=osb[:, et, :], start=(et == 0), stop=(et == 1)), reads=[K("ones"), K("osb", et)], writes=[K("ps_sum")], inc=(et == 1))
        for et in range(2):
            S.op('pe', lambda e: e.matmul(ps_sq[:], lhsT=ones[:], rhs=sq[:, et, :], start=(et == 0), stop=(et == 1)), reads=[K("ones"), K("sq", et)], writes=[K("ps_sq")], inc=(et == 1))
        S.op('dve', lambda e: e.tensor_scalar(out=mean[:], in0=ps_sum[:], scalar1=1.0 / 256.0, scalar2=None, op0=ALU.mult), reads=[K("ps_sum")], writes=[K("mean")])
        S.op('dve', lambda e: e.tensor_tensor(out=rs[:], in0=mean[:], in1=mean[:], op=ALU.mult), reads=[K("mean")], writes=[K("rs")])
        S.op('dve', lambda e: e.scalar_tensor_tensor(out=rs[:], in0=ps_sq[:], scalar=1.0 / 256.0, in1=rs[:], op0=ALU.mult, op1=ALU.subtract), reads=[K("ps_sq"), K("rs")], writes=[K("rs")])
        S.op('dve', lambda e: e.tensor_scalar(out=rs[:], in0=rs[:], scalar1=EPS, scalar2=None, op0=ALU.add), reads=[K("rs")], writes=[K("rs")])
        S.op('act', lambda e: e.activation(out=rs[:], in_=rs[:], func=AF.Sqrt), reads=[K("rs")], writes=[K("rs")])
        S.op('dve', lambda e: e.reciprocal(out=rs[:], in_=rs[:]), reads=[K("rs")], writes=[K("rs")])
        for et in range(2):
            S.op('dve', lambda e: e.tensor_tensor(out=osb[:, et, :], in0=osb[:, et, :], in1=mean[:], op=ALU.subtract), reads=[K("osb", et), K("mean"), K("ps_sum")], writes=[K("osb", et)])
            S.op('dve', lambda e: e.scalar_tensor_tensor(out=osb[:, et, :], in0=osb[:, et, :], scalar=pp[:, 1 + et:2 + et], in1=rs[:], op0=ALU.mult, op1=ALU.mult), reads=[K("osb", et), K("pp"), K("rs")], writes=[K("osb", et)])
            S.op('act', lambda e: e.activation(out=sq[:, et, :], in_=gin[b][:, et, :], func=AF.Silu), reads=[K("gin", b), K("sq", et), K("ps_sq")], writes=[K("sq", et)])
            S.op('dve', lambda e: e.tensor_tensor(out=yo[b][:, et, :], in0=osb[:, et, :], in1=sq[:, et, :], op=ALU.mult), reads=[K("osb", et), K("sq", et)], writes=[K("yo", b)])
        S.dma('sp', io['y'][:, :, c0:c0 + TT], yo[b][:], reads=[K("yo", b)])


EPS = 1e-6
NEG = -1.0e30


def dn_consts():
    f = np.arange(512) % 64
    p = np.arange(64)[:, None]
    cneg = np.where(f[None, :] >= p, 0.0, NEG).astype(np.float32)
    lneg = np.where(p > f[None, :], 0.0, NEG).astype(np.float32)
    ustrict = (f[None, :] > p).astype(np.float32)
    tri = (np.arange(64)[:, None] <= np.arange(64)[None, :]).astype(np.float32)
    eye8 = (f[None, :] == p).astype(np.float32)
    reset = np.ones((128, 512), np.float32); reset[:, ::64] = 0.0
    return dict(ident=np.eye(128, dtype=np.float32), ones=np.ones((128, 128), np.float32), cneg=cneg, lneg=lneg,
                ustrict=ustrict, tri=tri, eye8=eye8, reset=reset)


def emit_dn(S, nc, es, T, io, TT=512):
    A = lambda name, shape, dt=F32: es.enter_context(nc.sbuf_tensor("dn_" + name, shape, dt))
    K = lambda *a_: ("dn",) + a_
    NCH = TT // 64
    NC_ = T // 64
    bank = [es.enter_context(nc.psum_tensor("dn_bank%d" % i, [128, 512], F32)) for i in range(8)]
    BK = lambda i: K("bank", i)
    pp = A("pp", [128, 16]); ident = A("ident", [128, 128]); ones = A("ones", [128, 128])
    cneg = A("cneg", [64, TT]); lneg = A("lneg", [64, TT]); ustrict = A("ustrict", [64, TT]); tri = A("tri", [64, 64])
    eye8 = A("eye8", [64, TT]); reset = A("reset", [128, TT])
    abt = A("abt", [64, 2, NC_]); lat = A("lat", [64, NC_]); gt_ = A("gt", [64, NC_]); glast = A("glast", [128, NC_])
    betat = A("betat", [64, NC_]); bkt = A("bkt", [64, NC_]); tailt = A("tailt", [64, NC_]); cd = A("cd", [128, NC_]); Aexp = A("Aexp", [128, 1])
    for nm, t_ in (("pp", pp), ("ident", ident), ("ones", ones), ("cneg", cneg), ("lneg", lneg), ("ustrict", ustrict), ("tri", tri), ("eye8", eye8), ("reset", reset), ("ab_t", abt)):
        S.dma('sp', t_[:], io[nm], writes=[K(nm)])
    S.op('act', lambda e: e.activation(out=Aexp[:], in_=pp[:, 13:14], func=AF.Exp), reads=[K("pp")], writes=[K("Aexp")])
    S.op('act', lambda e: e.activation(out=lat[:], in_=abt[:, 0, :], func=AF.Exp, bias=pp[0:64, 14:15]), reads=[K("ab_t"), K("pp")], writes=[K("lat")])
    S.op('act', lambda e: e.activation(out=lat[:], in_=lat[:], func=AF.Ln, bias=1.0), reads=[K("lat")], writes=[K("lat")])
    S.op('dve', lambda e: e.tensor_scalar(out=lat[:], in0=lat[:], scalar1=Aexp[0:64, 0:1], scalar2=-1.0, op0=ALU.mult, op1=ALU.mult), reads=[K("lat"), K("Aexp")], writes=[K("lat")])
    S.op('pe', lambda e: e.matmul(bank[0][0:64, 0:NC_], lhsT=tri[:], rhs=lat[:], start=True, stop=True), reads=[K("tri"), K("lat")], writes=[BK(0)])
    S.op('pe', lambda e: e.matmul(bank[1][:, 0:NC_], lhsT=ones[0:64, :], rhs=lat[:], start=True, stop=True), reads=[K("ones"), K("lat")], writes=[BK(1)])
    S.op('act', lambda e: e.copy(out=gt_[:], in_=bank[0][0:64, 0:NC_]), reads=[BK(0)], writes=[K("gt")])
    S.op('act', lambda e: e.copy(out=glast[:], in_=bank[1][:, 0:NC_]), reads=[BK(1)], writes=[K("glast")])
    S.op('act', lambda e: e.activation(out=betat[:], in_=abt[:, 1, :], func=AF.Sigmoid), reads=[K("ab_t")], writes=[K("betat")])
    S.op('act', lambda e: e.activation(out=bkt[:], in_=gt_[:], func=AF.Exp), reads=[K("gt")], writes=[K("bkt")])
    S.op('dve', lambda e: e.tensor_tensor(out=bkt[:], in0=bkt[:], in1=betat[:], op=ALU.mult), reads=[K("bkt"), K("betat")], writes=[K("bkt")])
    S.op('dve', lambda e: e.tensor_tensor(out=tailt[:], in0=glast[0:64, :], in1=gt_[:], op=ALU.subtract), reads=[K("glast"), K("gt")], writes=[K("tailt")])
    S.op('act', lambda e: e.activation(out=tailt[:], in_=tailt[:], func=AF.Exp), reads=[K("tailt")], writes=[K("tailt")])
    S.op('act', lambda e: e.activation(out=cd[:], in_=glast[:], func=AF.Exp), reads=[K("glast")], writes=[K("cd")])
    xb = {w: [A("xb%s%d" % (w, i), [128, TT + 3]) for i in range(2)] for w in "qkv"}
    zin = [A("zin%d" % i, [128, TT]) for i in range(2)]
    abb = [A("abb%d" % i, [128, 2, TT]) for i in range(2)]
    cq = A("cq", [128, TT]); ck = A("ck", [128, TT]); cvv = A("cv", [128, TT])
    t1 = A("t1", [128, TT]); t2 = A("t2", [128, TT])
    qn = A("qn", [128, TT]); kn = A("kn", [128, TT]); qg = A("qg", [128, TT])
    gb = A("gb", [128, TT]); betab = A("betab", [128, TT])
    X = A("X", [64, TT]); decT = A("decT", [64, TT]); decL = A("decL", [64, TT]); bs = A("bs", [64, TT])
    Pm = [A("P%d" % i, [64, TT]) for i in range(2)]; PTm = [A("PT%d" % i, [64, TT]) for i in range(2)]
    Rm = A("R", [64, TT]); RTm = A("RT", [64, TT]); qkT = A("qkT", [64, TT])
    knT = A("knT", [64, NCH, 128]); vT = A("vT", [64, NCH, 128]); kb = A("kb", [64, NCH, 128]); ktl = A("ktl", [64, NCH, 128])
    usb = A("usb", [64, NCH, 128]); wsb = A("wsb", [128, TT]); vnew = [A("vnew%d" % i, [64, 128]) for i in range(2)]
    Sst = [A("S%d" % i, [128, 128]) for i in range(2)]
    yo = [A("yo%d" % i, [128, TT]) for i in range(2)]
    S.op('dve', lambda e: e.memset(Sst[0][:], 0.0), writes=[K("S", 0)])
    for w in "qkv":
        S.op('dve', lambda e: e.memset(xb[w][1][:, TT:TT + 3], 0.0), writes=[K("xbm" + w, 1)])
    scale = 128.0 ** -0.5
    nt = T // TT
    g = 0
    v3 = lambda ap: ap.rearrange("p (c j) -> p c j", j=64)
    for ti in range(nt):
        b = ti % 2
        pb = (ti + 1) % 2
        c0 = ti * TT
        n0 = ti * NCH
        for w in "qkv":
            S.dma('sp', xb[w][b][:, 3:TT + 3], io[w][:, c0:c0 + TT], writes=[K("xbm" + w, b)])
        S.dma('sp', zin[b][:], io['z'][:, c0:c0 + TT], writes=[K("zin", b)])
        S.dma('sp', abb[b][:], io['ab_b'][:, :, c0:c0 + TT], writes=[K("abb", b)])
        for wi_, (w, dst) in enumerate((("q", cq), ("k", ck), ("v", cvv))):
            eng = 'dve' if wi_ != 1 else 'pool'
            S.op('pool', lambda e: e.tensor_copy(out=xb[w][b][:, 0:3], in_=xb[w][pb][:, TT:TT + 3]), reads=[K("xbm" + w, pb), K("xb" + w, pb)], writes=[K("xb" + w, b)])
            XK = [K("xb" + w, b), K("xbm" + w, b), K("pp")]
            dk_ = K("c" + w)
            S.op(eng, lambda e: e.tensor_scalar(out=dst[:], in0=xb[w][b][:, 0:TT], scalar1=pp[:, 4 * wi_:4 * wi_ + 1], scalar2=None, op0=ALU.mult), reads=XK, writes=[dk_])
            for j in (1, 2, 3):
                if eng == 'dve':
                    S.op('dve', lambda e: e.scalar_tensor_tensor(out=dst[:], in0=xb[w][b][:, j:j + TT], scalar=pp[:, 4 * wi_ + j:4 * wi_ + j + 1], in1=dst[:], op0=ALU.mult, op1=ALU.add), reads=XK + [dk_], writes=[dk_])
                else:
                    S.op('pool', lambda e: e.tensor_scalar(out=t2[:], in0=xb[w][b][:, j:j + TT], scalar1=pp[:, 4 * wi_ + j:4 * wi_ + j + 1], scalar2=None, op0=ALU.mult), reads=XK, writes=[K("t2")])
                    S.op('pool', lambda e: e.tensor_tensor(out=dst[:], in0=dst[:], in1=t2[:], op=ALU.add), reads=[K("t2"), dk_], writes=[dk_])
            S.op('act', lambda e: e.activation(out=dst[:], in_=dst[:], func=AF.Silu), reads=[dk_], writes=[dk_])
        for (src, sk, dst, dk2, sc_) in ((cq, K("cq"), qn, K("qn"), scale), (ck, K("ck"), kn, K("kn"), 1.0)):
            S.op('act', lambda e: e.activation(out=t1[:], in_=src[:], func=AF.Square), reads=[sk], writes=[K("t1")])
            S.op('pe', lambda e: e.matmul(bank[5][:], lhsT=ones[:], rhs=t1[:], start=True, stop=True), reads=[K("ones"), K("t1")], writes=[BK(5)])
            S.op('dve', lambda e: e.tensor_scalar(out=t1[:], in0=bank[5][:], scalar1=EPS, scalar2=None, op0=ALU.add), reads=[BK(5)], writes=[K("t1")])
            S.op('act', lambda e: e.activation(out=t1[:], in_=t1[:], func=AF.Sqrt), reads=[K("t1")], writes=[K("t1")])
            S.op('dve', lambda e: e.reciprocal(out=t1[:], in_=t1[:]), reads=[K("t1")], writes=[K("t1")])
            S.op('dve', lambda e: e.scalar_tensor_tensor(out=dst[:], in0=src[:], scalar=sc_, in1=t1[:], op0=ALU.mult, op1=ALU.mult), reads=[sk, K("t1")], writes=[dk2])
        S.op('act', lambda e: e.activation(out=gb[:], in_=abb[b][:, 0, :], func=AF.Exp, bias=pp[:, 14:15]), reads=[K("abb", b), K("pp")], writes=[K("gb")])
        S.op('act', lambda e: e.activation(out=gb[:], in_=gb[:], func=AF.Ln, bias=1.0), reads=[K("gb")], writes=[K("gb")])
        S.op('dve', lambda e: e.tensor_scalar(out=gb[:], in0=gb[:], scalar1=Aexp[:, 0:1], scalar2=-1.0, op0=ALU.mult, op1=ALU.mult), reads=[K("gb"), K("Aexp")], writes=[K("gb")])
        S.op('dve', lambda e: e.tensor_tensor_scan(out=gb[:], data0=reset[:], data1=gb[:], initial=0.0, op0=ALU.mult, op1=ALU.add), reads=[K("reset"), K("gb")], writes=[K("gb")])
        S.op('act', lambda e: e.activation(out=betab[:], in_=abb[b][:, 1, :], func=AF.Sigmoid), reads=[K("abb", b)], writes=[K("betab")])
        S.op('act', lambda e: e.activation(out=t1[:], in_=gb[:], func=AF.Exp), reads=[K("gb")], writes=[K("t1")])
        S.op('dve', lambda e: e.tensor_tensor(out=qg[:], in0=qn[:], in1=t1[:], op=ALU.mult), reads=[K("qn"), K("t1")], writes=[K("qg")])
        S.op('dve', lambda e: e.tensor_tensor(out=v3(X[:]), in0=v3(gb[0:64, :]), in1=gt_[:, n0:n0 + NCH].unsqueeze(2).to_broadcast([64, NCH, 64]), op=ALU.subtract), reads=[K("gb"), K("gt")], writes=[K("X")])
        S.op('pool', lambda e: e.tensor_tensor(out=decT[:], in0=X[:], in1=cneg[:], op=ALU.add), reads=[K("X"), K("cneg")], writes=[K("decT")])
        S.op('act', lambda e: e.activation(out=decT[:], in_=decT[:], func=AF.Exp), reads=[K("decT")], writes=[K("decT")])
        S.op('pool', lambda e: e.tensor_tensor(out=decL[:], in0=X[:], in1=lneg[:], op=ALU.subtract), reads=[K("X"), K("lneg")], writes=[K("decL")])
        S.op('act', lambda e: e.activation(out=decL[:], in_=decL[:], func=AF.Exp, scale=-1.0), reads=[K("decL")], writes=[K("decL")])
        S.op('pool', lambda e: e.tensor_tensor(out=bs[:], in0=betab[0:64, :], in1=ustrict[:], op=ALU.mult), reads=[K("betab"), K("ustrict")], writes=[K("bs")])
        for n in range(NCH):
            cs = slice(n * 64, (n + 1) * 64)
            S.op('pe', lambda e: e.matmul(bank[1][0:64, cs], lhsT=kn[:, cs], rhs=kn[:, cs], start=True, stop=True), reads=[K("kn")], writes=[BK(1)], inc=(n == NCH - 1))
        for n in range(NCH):
            cs = slice(n * 64, (n + 1) * 64)
            S.op('pe', lambda e: e.matmul(bank[2][0:64, cs], lhsT=kn[:, cs], rhs=qn[:, cs], start=True, stop=True), reads=[K("kn"), K("qn")], writes=[BK(2)], inc=(n == NCH - 1))
        P, PT = Pm[0], PTm[0]
        S.op('dve', lambda e: e.tensor_tensor(out=P[:], in0=bank[1][0:64, :], in1=decT[:], op=ALU.mult), reads=[BK(1), K("decT")], writes=[K("P", 0)])
        S.op('dve', lambda e: e.tensor_tensor(out=P[:], in0=P[:], in1=bs[:], op=ALU.mult), reads=[K("P", 0), K("bs")], writes=[K("P", 0)])
        S.op('dve', lambda e: e.tensor_tensor(out=PT[:], in0=bank[1][0:64, :], in1=decL[:], op=ALU.mult), reads=[BK(1), K("decL")], writes=[K("PT", 0)])
        S.op('dve', lambda e: e.tensor_tensor(out=v3(PT[:]), in0=v3(PT[:]), in1=betat[:, n0:n0 + NCH].unsqueeze(2).to_broadcast([64, NCH, 64]), op=ALU.mult), reads=[K("PT", 0), K("betat")], writes=[K("PT", 0)])
        S.op('dve', lambda e: e.tensor_tensor(out=qkT[:], in0=bank[2][0:64, :], in1=decT[:], op=ALU.mult), reads=[BK(2), K("decT")], writes=[K("qkT")])
        S.op('pool', lambda e: e.tensor_tensor(out=Rm[:], in0=eye8[:], in1=P[:], op=ALU.subtract), reads=[K("eye8"), K("P", 0)], writes=[K("R")])
        S.op('pool', lambda e: e.tensor_tensor(out=RTm[:], in0=eye8[:], in1=PT[:], op=ALU.subtract), reads=[K("eye8"), K("PT", 0)], writes=[K("RT")])
        cur = 0
        for lvl in range(5):
            last = (lvl == 4)
            nx = 1 - cur
            for n in range(NCH):
                cs = slice(n * 64, (n + 1) * 64)
                S.op('pe', lambda e: e.matmul(bank[1][0:64, cs], lhsT=PTm[cur][:, cs], rhs=Pm[cur][:, cs], start=True, stop=True), reads=[K("PT", cur), K("P", cur)], writes=[BK(1)], inc=(n == NCH - 1))
            if not last:
                for n in range(NCH):
                    cs = slice(n * 64, (n + 1) * 64)
                    S.op('pe', lambda e: e.matmul(bank[2][0:64, cs], lhsT=Pm[cur][:, cs], rhs=PTm[cur][:, cs], start=True, stop=True), reads=[K("PT", cur), K("P", cur)], writes=[BK(2)], inc=(n == NCH - 1))
            S.op('act', lambda e: e.copy(out=Pm[nx][:], in_=bank[1][0:64, :]), reads=[BK(1)], writes=[K("P", nx)])
            if not last:
                S.op('dve', lambda e: e.tensor_copy(out=PTm[nx][:], in_=bank[2][0:64, :]), reads=[BK(2)], writes=[K("PT", nx)])
            for n in range(NCH):
                cs = slice(n * 64, (n + 1) * 64)
                S.op('pe', lambda e: e.matmul(bank[3][0:64, cs], lhsT=RTm[:, cs], rhs=Pm[nx][:, cs], start=True, stop=True), reads=[K("RT"), K("P", nx)], writes=[BK(3)], inc=(n == NCH - 1))
            if not last:
                for n in range(NCH):
                    cs = slice(n * 64, (n + 1) * 64)
                    S.op('pe', lambda e: e.matmul(bank[4][0:64, cs], lhsT=Pm[nx][:, cs], rhs=RTm[:, cs], start=True, stop=True), reads=[K("RT"), K("P", nx)], writes=[BK(4)], inc=(n == NCH - 1))
            S.op('dve', lambda e: e.tensor_tensor(out=Rm[:], in0=Rm[:], in1=bank[3][0:64, :], op=ALU.add), reads=[K("R"), BK(3)], writes=[K("R")])
            if not last:
                S.op('dve', lambda e: e.tensor_tensor(out=RTm[:], in0=RTm[:], in1=bank[4][0:64, :], op=ALU.add), reads=[K("RT"), BK(4)], writes=[K("RT")])
            cur = nx
        for (src, sk, dst, dk2) in ((kn, K("kn"), knT, K("knT")), (cvv, K("cv"), vT, K("vT"))):
            for hf in range(2):
                for n4 in range(4):
                    n = hf * 4 + n4
                    S.op('pe', lambda e: e.transpose(out=bank[0][0:64, n4 * 128:(n4 + 1) * 128], in_=src[:, n * 64:(n + 1) * 64], identity=ident[:]), reads=[sk, K("ident")], writes=[BK(0)], inc=(n4 == 3))
                S.op('act', lambda e: e.copy(out=dst[:, hf * 4:(hf + 1) * 4, :].rearrange("p c d -> p (c d)"), in_=bank[0][0:64, :]), reads=[BK(0)], writes=[dk2])
        bc = lambda t_: t_[:, n0:n0 + NCH].unsqueeze(2).to_broadcast([64, NCH, 128])
        S.op('pool', lambda e: e.tensor_tensor(out=kb[:], in0=knT[:], in1=bc(bkt), op=ALU.mult), reads=[K("knT"), K("bkt")], writes=[K("kb")])
        S.op('pool', lambda e: e.tensor_tensor(out=ktl[:], in0=knT[:], in1=bc(tailt), op=ALU.mult), reads=[K("knT"), K("tailt")], writes=[K("ktl")])
        S.op('dve', lambda e: e.tensor_tensor(out=vT[:], in0=vT[:], in1=bc(betat), op=ALU.mult), reads=[K("vT"), K("betat")], writes=[K("vT")])
        for hf in range(2):
            for n4 in range(4):
                n = hf * 4 + n4
                cs = slice(n * 64, (n + 1) * 64)
                S.op('pe', lambda e: e.matmul(bank[1 + hf][0:64, n4 * 128:(n4 + 1) * 128], lhsT=Rm[:, cs], rhs=vT[:, n, :], start=True, stop=True), reads=[K("R"), K("vT")], writes=[BK(1 + hf)], inc=(n4 == 3))
            S.op('act', lambda e: e.copy(out=usb[:, hf * 4:(hf + 1) * 4, :].rearrange("p c d -> p (c d)"), in_=bank[1 + hf][0:64, :]), reads=[BK(1 + hf)], writes=[K("usb")])
        for n in range(NCH):
            cs = slice(n * 64, (n + 1) * 64)
            S.op('pe', lambda e: e.matmul(bank[0][:, cs], lhsT=kb[:, n, :], rhs=Rm[:, cs], start=True, stop=True), reads=[K("kb"), K("R")], writes=[BK(0)], inc=(n == NCH - 1))
        S.op('act', lambda e: e.copy(out=wsb[:], in_=bank[0][:]), reads=[BK(0)], writes=[K("wsb")])
        for n in range(NCH):
            cs = slice(n * 64, (n + 1) * 64)
            sc, sn = g % 2, (g + 1) % 2
            vb_ = g % 2
            S.op('pe', lambda e: e.matmul(bank[7][0:64, 0:128], lhsT=wsb[:, cs], rhs=Sst[sc][:], start=True, stop=True), reads=[K("wsb"), K("S", sc)], writes=[K("ps_ws")])
            S.op('dve', lambda e: e.tensor_tensor(out=vnew[vb_][:], in0=usb[:, n, :], in1=bank[7][0:64, 0:128], op=ALU.subtract), reads=[K("usb"), K("ps_ws")], writes=[K("vnew", vb_)])
            S.op('pe', lambda e: e.matmul(bank[6][:, cs], lhsT=Sst[sc][:], rhs=qg[:, cs], start=True, stop=False), reads=[K("S", sc), K("qg")], writes=[BK(6)], inc=False)
            S.op('pe', lambda e: e.matmul(bank[6][:, cs], lhsT=vnew[vb_][:], rhs=qkT[:, cs], start=False, stop=True), reads=[K("vnew", vb_), K("qkT")], writes=[BK(6)], inc=False)
            S.op('pe', lambda e: e.matmul(bank[7][:, 128:256], lhsT=ktl[:, n, :], rhs=vnew[vb_][:], start=True, stop=True), reads=[K("ktl"), K("vnew", vb_)], writes=[K("ps_S")])
            S.op('dve', lambda e: e.scalar_tensor_tensor(out=Sst[sn][:], in0=Sst[sc][:], scalar=cd[:, n0 + n:n0 + n + 1], in1=bank[7][:, 128:256], op0=ALU.mult, op1=ALU.add), reads=[K("S", sc), K("cd"), K("ps_S")], writes=[K("S", sn)])
            g += 1
        S.op('act', lambda e: e.activation(out=t1[:], in_=bank[6][:], func=AF.Square), reads=[BK(6)], writes=[K("t1")])
        S.op('pe', lambda e: e.matmul(bank[5][:], lhsT=ones[:], rhs=t1[:], start=True, stop=True), reads=[K("ones"), K("t1")], writes=[BK(5)])
        S.op('dve', lambda e: e.tensor_scalar(out=t1[:], in0=bank[5][:], scalar1=1.0 / 128.0, scalar2=EPS, op0=ALU.mult, op1=ALU.add), reads=[BK(5)], writes=[K("t1")])
        S.op('act', lambda e: e.activation(out=t1[:], in_=t1[:], func=AF.Sqrt), reads=[K("t1")], writes=[K("t1")])
        S.op('dve', lambda e: e.reciprocal(out=t1[:], in_=t1[:]), reads=[K("t1")], writes=[K("t1")])
        S.op('dve', lambda e: e.scalar_tensor_tensor(out=t1[:], in0=bank[6][:], scalar=pp[:, 12:13], in1=t1[:], op0=ALU.mult, op1=ALU.mult), reads=[BK(6), K("pp"), K("t1")], writes=[K("t1")])
        S.op('act', lambda e: e.activation(out=t2[:], in_=zin[b][:], func=AF.Silu), reads=[K("zin", b)], writes=[K("t2")])
        S.op('dve', lambda e: e.tensor_tensor(out=yo[b][:], in0=t1[:], in1=t2[:], op=ALU.mult), reads=[K("t1"), K("t2")], writes=[K("yo", b)])
        S.dma('sp', io['y'][:, c0:c0 + TT], yo[b][:], reads=[K("yo", b)])


EPS = 1e-6
TS = 512


class Gemm:
    def __init__(self, S, nc, es, KC=16, NS=3, NB=3):
        self.S, self.KC, self.NS, self.NB = S, KC, NS, NB
        self.st = [es.enter_context(nc.sbuf_tensor("g_st%d" % i, [128, KC, 128], F32)) for i in range(NS)]
        self.bf = [es.enter_context(nc.sbuf_tensor("g_bf%d" % i, [128, KC, 128], BF16)) for i in range(NB)]
        self.si = 0
        self.bi = 0
        self.ci = 0
        self.cast_engs = ['pool', 'dve', 'act']

    def tile(self, Wn, KT, mov, ps_ap, ps_key):
        S, KC = self.S, self.KC
        for kc0 in range(0, KT, KC):
            kc = min(KC, KT - kc0)
            sb = self.si % self.NS; self.si += 1
            bb = self.bi % self.NB; self.bi += 1
            st, bf = self.st[sb], self.bf[bb]
            S.dma('sp', st[:, 0:kc, :], Wn[:, kc0:kc0 + kc, :], writes=[("wst", sb)])
            eng = self.cast_engs[self.ci % len(self.cast_engs)]; self.ci += 1
            if eng == 'act':
                S.op('act', lambda e: e.copy(out=bf[:, 0:kc, :], in_=st[:, 0:kc, :]), reads=[("wst", sb)], writes=[("wbf", bb)])
            else:
                S.op(eng, lambda e: e.tensor_copy(out=bf[:, 0:kc, :], in_=st[:, 0:kc, :]), reads=[("wst", sb)], writes=[("wbf", bb)])
            for k in range(kc):
                kt = kc0 + k
                ap, key = mov(kt)
                S.op('pe', lambda e: e.matmul(ps_ap, lhsT=bf[:, k, :], rhs=ap, start=(kt == 0), stop=(kt == KT - 1)),
                     reads=[("wbf", bb), key], writes=[ps_key], inc=(k == kc - 1))


class Norm:
    def __init__(self, S, nc, es, ones, ps_ss):
        self.S = S
        self.ones = ones
        self.ps = ps_ss
        self.sq = [es.enter_context(nc.sbuf_tensor("n_sq%d" % i, [128, TS], F32)) for i in range(2)]
        self.rs = es.enter_context(nc.sbuf_tensor("n_rs", [128, TS], F32))
        self.i = 0

    def add(self, kt, KT, src_ap, src_key):
        S = self.S
        b = self.i % 2; self.i += 1
        sq = self.sq[b]
        S.op('act', lambda e: e.activation(out=sq[:], in_=src_ap, func=AF.Square), reads=[src_key], writes=[("n_sq", b)])
        S.op('pe', lambda e: e.matmul(self.ps[:], lhsT=self.ones[:], rhs=sq[:], start=(kt == 0), stop=(kt == KT - 1)),
             reads=[("ones",), ("n_sq", b)], writes=[("ps_ss",)])

    def finish(self, D):
        S, rs = self.S, self.rs
        S.op('dve', lambda e: e.tensor_scalar(out=rs[:], in0=self.ps[:], scalar1=1.0 / D, scalar2=EPS, op0=ALU.mult, op1=ALU.add), reads=[("ps_ss",)], writes=[("n_rs",)])
        S.op('act', lambda e: e.activation(out=rs[:], in_=rs[:], func=AF.Sqrt), reads=[("n_rs",)], writes=[("n_rs",)])
        S.op('dve', lambda e: e.reciprocal(out=rs[:], in_=rs[:]), reads=[("n_rs",)], writes=[("n_rs",)])

    def apply(self, out_ap, out_key, src_ap, src_key, gcol, eng='dve'):
        self.S.op('dve', lambda e: e.scalar_tensor_tensor(out=out_ap, in0=src_ap, scalar=gcol, in1=self.rs[:], op0=ALU.mult, op1=ALU.mult),
                  reads=[src_key, ("n_rs",), ("vec",)], writes=[out_key])


def build_A(NTOK, NT_OUT, KT=32):
    nc = bass.Bass("TRN2", target_bir_lowering=False)
    D = lambda n, s, k="ExternalInput": nc.dram_tensor(n, s, F32, kind=k).ap()
    xT = D("xT", [KT * 128, NTOK]); gm = D("gm", [128, KT]); WA = D("WA", [NT_OUT, 128, KT, 128]); ones_d = D("ones", [128, 128])
    uT = D("uT", [NT_OUT * 128, NTOK], "ExternalOutput")
    xv = xT.rearrange("(kt p) t -> p kt t", p=128)
    with ExitStack() as es:
        S = Sch(nc, es)
        A = lambda name, shape, dt=F32: es.enter_context(nc.sbuf_tensor(name, shape, dt))
        ones = A("ones_t", [128, 128]); gmt = A("gmt", [128, KT]); hT = A("hT", [128, KT, TS], BF16)
        xt = [A("xt%d" % i, [128, TS]) for i in range(4)]
        ot = [A("ot%d" % i, [128, TS]) for i in range(3)]
        ps_ss = es.enter_context(nc.psum_tensor("ps_ss", [128, TS], F32))
        ps = [es.enter_context(nc.psum_tensor("ps%d" % i, [128, TS], F32)) for i in range(6)]
        S.dma('sp', ones[:], ones_d, writes=[("ones",)])
        S.dma('sp', gmt[:], gm, writes=[("vec",)])
        G = Gemm(S, nc, es)
        N_ = Norm(S, nc, es, ones, ps_ss)
        xi = 0; oi = 0; pi = 0
        for p in range(NTOK // TS):
            ts = slice(p * TS, (p + 1) * TS)
            for kt in range(KT):
                r = xi % 4; xi += 1
                S.dma('sp', xt[r][:], xv[:, kt, ts], writes=[("xt", r)])
                N_.add(kt, KT, xt[r][:], ("xt", r))
            N_.finish(KT * 128.0)
            for kt in range(KT):
                r = xi % 4; xi += 1
                S.dma('sp', xt[r][:], xv[:, kt, ts], writes=[("xt", r)])
                N_.apply(hT[:, kt, :], ("hT", kt), xt[r][:], ("xt", r), gmt[:, kt:kt + 1])
            for n in range(NT_OUT):
                pb = pi % 6; pi += 1
                G.tile(WA[n], KT, lambda kt: (hT[:, kt, :], ("hT", kt)), ps[pb][:], ("ps", pb))
                ob = oi % 3; oi += 1
                S.op('act', lambda e: e.copy(out=ot[ob][:], in_=ps[pb][:]), reads=[("ps", pb)], writes=[("ot", ob)])
                S.dma('act', uT[n * 128:(n + 1) * 128, ts], ot[ob][:], reads=[("ot", ob)])
        S.finish()
        print("A ops", S.nops, "waits", S.nwaits)
    return nc


def build_C(NTOK, final, KT=32, NB=4, KY=8, FT=86):
    nc = bass.Bass("TRN2", target_bir_lowering=False)
    D = lambda n, s, k="ExternalInput": nc.dram_tensor(n, s, F32, kind=k).ap()
    xT = D("xT", [KT * 128, NTOK]); yT = D("yT", [NB * KY * 128, NTOK])
    vec = D("vec", [128, 3 * KT + NB * KT])
    Wg = D("Wg", [NB * KT, 128, KT, 128]); Pb = D("Pb", [NB * KT, 128, KY, 128]); Wo = D("Wo", [KT, 128, KT, 128])
    Wfg = D("Wfg", [FT, 128, KT, 128]); Wfu = D("Wfu", [FT, 128, KT, 128]); Wd = D("Wd", [KT, 128, FT, 128]); ones_d = D("ones", [128, 128])
    out = D("out", [KT * 128, NTOK], "ExternalOutput")
    xv = xT.rearrange("(kt p) t -> p kt t", p=128)
    yv = yT.rearrange("(kt p) t -> p kt t", p=128)
    ov = out.rearrange("(kt p) t -> p kt t", p=128)
    with ExitStack() as es:
        S = Sch(nc, es)
        A = lambda name, shape, dt=F32: es.enter_context(nc.sbuf_tensor(name, shape, dt))
        ones = A("ones_t", [128, 128]); vt = A("vt", [128, 3 * KT + NB * KT])
        hT = A("hT", [128, KT, TS], BF16); big = A("big", [128, max(FT, NB * KY + KT), TS], BF16)
        yTb = lambda kt: big[:, kt, :]
        mT = lambda kt: big[:, NB * KY + kt, :]
        xt = [A("xt%d" % i, [128, TS]) for i in range(4)]
        gt = [A("gt%d" % i, [128, TS]) for i in range(2)]
        macc = A("macc", [128, TS]); tmp = A("tmp", [128, TS])
        ot = [A("ot%d" % i, [128, TS]) for i in range(3)]
        ps_ss = es.enter_context(nc.psum_tensor("ps_ss", [128, TS], F32))
        ps = [es.enter_context(nc.psum_tensor("ps%d" % i, [128, TS], F32)) for i in range(7)]
        S.dma('sp', ones[:], ones_d, writes=[("ones",)])
        S.dma('sp', vt[:], vec, writes=[("vec",)])
        G = Gemm(S, nc, es)
        N_ = Norm(S, nc, es, ones, ps_ss)
        st = dict(xi=0, oi=0, pi=0, gi=0)

        def nxt(k, m):
            v = st[k] % m; st[k] += 1
            return v
        for p in range(NTOK // TS):
            ts = slice(p * TS, (p + 1) * TS)
            for kt in range(KT):
                r = nxt('xi', 4)
                S.dma('act', xt[r][:], xv[:, kt, ts], writes=[("xt", r)])
                N_.add(kt, KT, xt[r][:], ("xt", r))
            N_.finish(KT * 128.0)
            for kt in range(KT):
                r = nxt('xi', 4)
                S.dma('act', xt[r][:], xv[:, kt, ts], writes=[("xt", r)])
                N_.apply(hT[:, kt, :], ("hT", kt), xt[r][:], ("xt", r), vt[:, kt:kt + 1])
            for kt in range(NB * KY):
                r = nxt('xi', 4)
                S.dma('act', xt[r][:], yv[:, kt, ts], writes=[("xt", r)])
                S.op('pool', lambda e: e.tensor_copy(out=yTb(kt), in_=xt[r][:]), reads=[("xt", r)], writes=[("big", kt)])
            for c in range(KT):
                for b in range(NB):
                    pg = nxt('pi', 7)
                    G.tile(Wg[b * KT + c], KT, lambda kt: (hT[:, kt, :], ("hT", kt)), ps[pg][:], ("ps", pg))
                    gi = nxt('gi', 2)
                    bcol = 3 * KT + b * KT + c
                    S.op('act', lambda e: e.activation(out=gt[gi][:], in_=ps[pg][:], func=AF.Sigmoid, bias=vt[:, bcol:bcol + 1]), reads=[("ps", pg), ("vec",)], writes=[("gt", gi)])
                    pz = nxt('pi', 7)
                    G.tile(Pb[b * KT + c], KY, lambda kt: (yTb(b * KY + kt), ("big", b * KY + kt)), ps[pz][:], ("ps", pz))
                    if b == 0:
                        S.op('dve', lambda e: e.tensor_tensor(out=macc[:], in0=ps[pz][:], in1=gt[gi][:], op=ALU.mult), reads=[("ps", pz), ("gt", gi)], writes=[("macc",)])
                    else:
                        S.op('dve', lambda e: e.tensor_tensor(out=tmp[:], in0=ps[pz][:], in1=gt[gi][:], op=ALU.mult), reads=[("ps", pz), ("gt", gi)], writes=[("tmp",)])
                        if b < NB - 1:
                            S.op('pool', lambda e: e.tensor_tensor(out=macc[:], in0=macc[:], in1=tmp[:], op=ALU.add), reads=[("macc",), ("tmp",)], writes=[("macc",)])
                        else:
                            S.op('pool', lambda e: e.tensor_tensor(out=mT(c), in0=macc[:], in1=tmp[:], op=ALU.add), reads=[("macc",), ("tmp",)], writes=[("big", NB * KY + c)])
            for c in range(KT):
                pg = nxt('pi', 7)
                G.tile(Wo[c], KT, lambda kt: (mT(kt), ("big", NB * KY + kt)), ps[pg][:], ("ps", pg))
                r = nxt('xi', 4)
                S.dma('act', xt[r][:], xv[:, c, ts], writes=[("xt", r)])
                ob = nxt('oi', 3)
                S.op('dve', lambda e: e.tensor_tensor(out=ot[ob][:], in0=ps[pg][:], in1=xt[r][:], op=ALU.add), reads=[("ps", pg), ("xt", r)], writes=[("ot", ob)])
                S.dma('act', ov[:, c, ts], ot[ob][:], reads=[("ot", ob)], writes=[("out", c)])
                N_.add(c, KT, ot[ob][:], ("ot", ob))
            N_.finish(KT * 128.0)
            for kt in range(KT):
                r = nxt('xi', 4)
                S.dma('act', xt[r][:], ov[:, kt, ts], reads=[("out", kt)], writes=[("xt", r)])
                N_.apply(hT[:, kt, :], ("hT", kt), xt[r][:], ("xt", r), vt[:, KT + kt:KT + kt + 1])
            for f in range(FT):
                pa = nxt('pi', 7)
                G.tile(Wfg[f], KT, lambda kt: (hT[:, kt, :], ("hT", kt)), ps[pa][:], ("ps", pa))
                gi = nxt('gi', 2)
                S.op('act', lambda e: e.activation(out=gt[gi][:], in_=ps[pa][:], func=AF.Silu), reads=[("ps", pa)], writes=[("gt", gi)])
                pu = nxt('pi', 7)
                G.tile(Wfu[f], KT, lambda kt: (hT[:, kt, :], ("hT", kt)), ps[pu][:], ("ps", pu))
                S.op('dve', lambda e: e.tensor_tensor(out=big[:, f, :], in0=ps[pu][:], in1=gt[gi][:], op=ALU.mult), reads=[("ps", pu), ("gt", gi)], writes=[("big", f)])
            for c in range(KT):
                pg = nxt('pi', 7)
                G.tile(Wd[c], FT, lambda kt: (big[:, kt, :], ("big", kt)), ps[pg][:], ("ps", pg))
                r = nxt('xi', 4)
                S.dma('act', xt[r][:], ov[:, c, ts], reads=[("out", c)], writes=[("xt", r)])
                ob = nxt('oi', 3)
                S.op('dve', lambda e: e.tensor_tensor(out=ot[ob][:], in0=ps[pg][:], in1=xt[r][:], op=ALU.add), reads=[("ps", pg), ("xt", r)], writes=[("ot", ob)])
                S.dma('act', ov[:, c, ts], ot[ob][:], reads=[("ot", ob)], writes=[("out", c)])
                if final:
                    N_.add(c, KT, ot[ob][:], ("ot", ob))
            if final:
                N_.finish(KT * 128.0)
                for kt in range(KT):
                    r = nxt('xi', 4)
                    S.dma('act', xt[r][:], ov[:, kt, ts], reads=[("out", kt)], writes=[("xt", r)])
                    ob = nxt('oi', 3)
                    N_.apply(ot[ob][:], ("ot", ob), xt[r][:], ("xt", r), vt[:, 2 * KT + kt:2 * KT + kt + 1])
                    S.dma('act', ov[:, kt, ts], ot[ob][:], reads=[("ot", ob)], writes=[("out", kt)])
        S.finish()
        print("C ops", S.nops, "waits", S.nwaits)
    return nc


def tile_w(W):
    K, N = W.shape
    NP = ((N + 127) // 128) * 128
    if NP != N:
        W = np.concatenate([W, np.zeros((K, NP - N), W.dtype)], axis=1)
    return np.ascontiguousarray(W.reshape(K // 128, 128, NP // 128, 128).transpose(2, 1, 0, 3))


def build_B(T):
    nc = bass.Bass("TRN2", target_bir_lowering=False)
    D = lambda n, s, k="ExternalInput": nc.dram_tensor(n, s, F32, kind=k).ap()
    NC_ = T // 64
    cst = dict(ident=D("c_ident", [128, 128]), ones=D("c_ones", [128, 128]), reset=D("c_reset", [128, 512]))
    io_lru = dict(lx=D("l_x", [128, T]), lg=D("l_g", [128, T]), pp=D("l_pp", [128, 16]), wr=D("l_wr", [128, 128]), wi=D("l_wi", [128, 128]),
                  y=D("y_lru", [128, T], "ExternalOutput"))
    io_hg = dict(q=D("h_q", [128, T]), f=D("h_f", [128, T]), g=D("h_g", [128, T]), v=D("h_v", [64, NC_, 128]), pp=D("h_pp", [128, 8]),
                 maskc=D("h_maskc", [64, 512]), y=D("y_hg", [128, T], "ExternalOutput"), **cst)
    io_ret = dict(q=D("r_q", [128, 2, T]), k=D("r_k", [128, 2, T]), g=D("r_g", [128, 2, T]), v=D("r_v", [64, NC_, 256]), pp=D("r_pp", [128, 4]),
                  cos=D("r_cos", [128, T]), sin=D("r_sin", [128, T]), qdec=D("r_qdec", [128, 512]), kdec=D("r_kdec", [128, 512]), dmask=D("r_dmask", [64, 512]),
                  ident=cst["ident"], ones=cst["ones"], y=D("y_ret", [128, 2, T], "ExternalOutput"))
    io_dn = dict(q=D("d_q", [128, T]), k=D("d_k", [128, T]), v=D("d_v", [128, T]), z=D("d_z", [128, T]), ab_b=D("d_abb", [128, 2, T]), ab_t=D("d_abt", [64, 2, NC_]),
                 pp=D("d_pp", [128, 16]), cneg=D("d_cneg", [64, 512]), lneg=D("d_lneg", [64, 512]), ustrict=D("d_ustrict", [64, 512]), tri=D("d_tri", [64, 64]),
                 eye8=D("d_eye8", [64, 512]), y=D("y_dn", [128, T], "ExternalOutput"), **cst)
    with ExitStack() as es:
        S = Sch(nc, es)
        for emit, io in ((emit_dn, io_dn), (emit_ret, io_ret), (emit_hg, io_hg), (emit_lru, io_lru)):
            with ExitStack() as es2:
                emit(S, nc, es2, T, io)
                S.barrier()
        S.finish()
        print("B ops", S.nops, "waits", S.nwaits)
    return nc


D_MODEL = 4096
SEQ = 16384
NCORE = 8
NTOK = SEQ // NCORE
D_MIX = 1024
N_MIXCOLS = 14 * D_MIX + 16
_CACHE = {}


def _prog(name, fn):
    if name not in _CACHE:
        _CACHE[name] = fn()
    return _CACHE[name]


def _run(nc, in_maps):
    res = run_bass_kernel_spmd(nc, in_maps, core_ids=list(range(NCORE)))
    return res.results


def _vtile(v):
    return np.ascontiguousarray(np.asarray(v, np.float32).reshape(-1, 128).T)


def _tok_major(a, width):
    T = a.shape[1]
    return np.ascontiguousarray(a.T.reshape(T // 64, 64, width).transpose(1, 0, 2))


def kernel(**inputs):
    f32 = np.float32
    x = np.asarray(inputs["x"], f32)[0]
    T = SEQ
    xT = np.ascontiguousarray(x.T)
    ones = np.ones((128, 128), f32)
    hgc = hg_consts()
    dnc = dn_consts()
    retc = [ret_consts(h, T) for h in range(4)]
    norm_mix = np.asarray(inputs["norm_mix"], f32); norm_ffn = np.asarray(inputs["norm_ffn"], f32); norm_final = np.asarray(inputs["norm_final"], f32)
    merge_bias = np.asarray(inputs["merge_bias"], f32)
    for l in range(2):
        w_in = np.asarray(inputs["w_in"][l], f32)
        WA = tile_w(w_in[:, :N_MIXCOLS])
        NT_A = WA.shape[0]
        ncA = _prog("A", lambda: build_A(NTOK, NT_A))
        gm = _vtile(norm_mix[l])
        maps = [dict(xT=np.ascontiguousarray(xT[:, c * NTOK:(c + 1) * NTOK]), gm=gm, WA=WA, ones=ones) for c in range(NCORE)]
        resA = _run(ncA, maps)
        del WA, maps
        uT = np.concatenate([r["uT"] for r in resA], axis=1)
        del resA
        blk = lambda j, c, w=128: uT[j * D_MIX + c * w: j * D_MIX + (c + 1) * w, :]
        ncB = _prog("B", lambda: build_B(T))
        cw = np.asarray(inputs["lru_conv_w"][l], f32); dcw = np.asarray(inputs["dn_conv_w"][l], f32)
        maps = []
        for c in range(NCORE):
            rh = c // 2
            sl = slice(c * 128, (c + 1) * 128)
            m = dict(c_ident=np.eye(128, dtype=f32), c_ones=ones, c_reset=hgc["reset"])
            lpp = np.zeros((128, 16), f32)
            lpp[:, 0:4] = cw[:, sl].T; lpp[:, 4] = inputs["lru_conv_b"][l][sl]; lpp[:, 5] = inputs["lru_b_r"][l][sl]
            lpp[:, 6] = inputs["lru_b_i"][l][sl]; lpp[:, 7] = inputs["lru_a"][l][sl]
            m.update(l_x=np.ascontiguousarray(blk(4, c)), l_g=np.ascontiguousarray(blk(5, c)), l_pp=lpp,
                     l_wr=np.ascontiguousarray(np.asarray(inputs["lru_w_r"][l][c], f32)), l_wi=np.ascontiguousarray(np.asarray(inputs["lru_w_i"][l][c], f32)))
            hpp = np.zeros((128, 8), f32)
            hpp[:, 0] = inputs["hg_lb_logits"][0][sl]; hpp[:, 1] = inputs["hg_lb_logits"][1][sl]; hpp[:, 2] = float(l); hpp[:, 3] = inputs["hg_norm"][l][sl]
            m.update(h_q=np.ascontiguousarray(blk(6, c)), h_f=np.ascontiguousarray(blk(7, c)), h_g=np.ascontiguousarray(blk(9, c)),
                     h_v=_tok_major(blk(8, c), 128), h_pp=hpp, h_maskc=hgc["maskc"])
            rc, g64 = retc[rh]
            rpp = np.zeros((128, 4), f32)
            gn = np.asarray(inputs["ret_gn"][l], f32)[rh * 256:(rh + 1) * 256]
            rpp[:, 0] = g64; rpp[:, 1] = gn[:128]; rpp[:, 2] = gn[128:]
            fm2 = lambda j: np.ascontiguousarray(blk(j, rh, 256).reshape(2, 128, T).transpose(1, 0, 2))
            m.update(r_q=fm2(0), r_k=fm2(1), r_g=fm2(3), r_v=_tok_major(blk(2, rh, 256), 256), r_pp=rpp,
                     r_cos=rc["cos"], r_sin=rc["sin"], r_qdec=rc["qdec"], r_kdec=rc["kdec"], r_dmask=rc["dmask"])
            dpp = np.zeros((128, 16), f32)
            dpp[:, 0:4] = dcw[:, c * 128:(c + 1) * 128].T; dpp[:, 4:8] = dcw[:, 1024 + c * 128:1024 + (c + 1) * 128].T
            dpp[:, 8:12] = dcw[:, 2048 + c * 128:2048 + (c + 1) * 128].T
            dpp[:, 12] = inputs["dn_norm"][l]; dpp[:, 13] = inputs["dn_a_log"][l][c]; dpp[:, 14] = inputs["dn_dt_bias"][l][c]
            ab = np.stack([uT[14 * D_MIX + c, :], uT[14 * D_MIX + 8 + c, :]], 0)
            m.update(d_q=np.ascontiguousarray(blk(10, c)), d_k=np.ascontiguousarray(blk(11, c)), d_v=np.ascontiguousarray(blk(12, c)), d_z=np.ascontiguousarray(blk(13, c)),
                     d_abb=np.ascontiguousarray(np.broadcast_to(ab[None], (128, 2, T))), d_abt=np.ascontiguousarray(ab.reshape(2, T // 64, 64).transpose(2, 0, 1)),
                     d_pp=dpp, d_cneg=dnc["cneg"], d_lneg=dnc["lneg"], d_ustrict=dnc["ustrict"], d_tri=dnc["tri"], d_eye8=dnc["eye8"])
            maps.append(m)
        del uT
        resB = _run(ncB, maps)
        del maps
        yT = np.empty((4 * D_MIX, T), f32)
        for c in range(NCORE):
            r = resB[c]
            if c % 2 == 0:
                rh = c // 2
                yT[rh * 256:(rh + 1) * 256, :] = r["y_ret"].transpose(1, 0, 2).reshape(256, T)
            yT[1 * D_MIX + c * 128:1 * D_MIX + (c + 1) * 128, :] = r["y_lru"]
            yT[2 * D_MIX + c * 128:2 * D_MIX + (c + 1) * 128, :] = r["y_hg"]
            yT[3 * D_MIX + c * 128:3 * D_MIX + (c + 1) * 128, :] = r["y_dn"]
        del resB
        final = (l == 1)
        ncC = _prog("C%d" % int(final), lambda: build_C(NTOK, final))
        vec = np.ascontiguousarray(np.concatenate([_vtile(norm_mix[l]), _vtile(norm_ffn[l]), _vtile(norm_final), _vtile(merge_bias[l])], axis=1))
        Wg = tile_w(w_in[:, N_MIXCOLS:])
        del w_in
        wb = np.asarray(inputs["w_branch"][l], f32)
        Pb = np.concatenate([tile_w(wb[b]) for b in range(4)], 0)
        Wo = tile_w(np.asarray(inputs["w_out"][l], f32))
        Wfg = tile_w(np.asarray(inputs["w_ffn_gate"][l], f32)); Wfu = tile_w(np.asarray(inputs["w_ffn_up"][l], f32)); Wd = tile_w(np.asarray(inputs["w_ffn_down"][l], f32))
        maps = [dict(xT=np.ascontiguousarray(xT[:, c * NTOK:(c + 1) * NTOK]), yT=np.ascontiguousarray(yT[:, c * NTOK:(c + 1) * NTOK]), vec=vec,
                     Wg=Wg, Pb=Pb, Wo=Wo, Wfg=Wfg, Wfu=Wfu, Wd=Wd, ones=ones) for c in range(NCORE)]
        del yT
        resC = _run(ncC, maps)
        del maps, Wg, Pb, Wo, Wfg, Wfu, Wd
        xT = np.concatenate([r["out"] for r in resC], axis=1)
        del resC
    return np.ascontiguousarray(xT.T)[None].astype(f32)
```

```python
import math
import numpy as np
from contextlib import ExitStack
import concourse.bass as bass
import concourse.mybir as mybir
from concourse.bass_utils import run_bass_kernel_spmd

F32 = mybir.dt.float32
BF16 = mybir.dt.bfloat16
AF = mybir.ActivationFunctionType
ALU = mybir.AluOpType
AX = mybir.AxisListType


class Sch:
    NDQ = 6

    def __init__(self, nc, es, same_eng_sync=True):
        self.nc = nc
        self.E = {'pe': nc.tensor, 'dve': nc.vector, 'act': nc.scalar,
                  'pool': nc.gpsimd, 'sp': nc.sync}
        self.sem = {}
        self.cnt = {}
        for e in ['pe', 'dve', 'act', 'pool']:
            self.sem[e] = es.enter_context(nc.semaphore("s_" + e))
            self.cnt[e] = 0
        self.dq = {}
        self.dqi = {}
        for q in ['sp', 'pool', 'act']:
            names = []
            for i in range(self.NDQ):
                n = "d_%s%d" % (q, i)
                self.sem[n] = es.enter_context(nc.semaphore(n))
                self.cnt[n] = 0
                names.append(n)
            self.dq[q] = names
            self.dqi[q] = 0
        self.seen = {e: {} for e in self.E}
        self.lastw = {}
        self.readers = {}
        self.pend = {e: ([], []) for e in self.E}
        self.same = same_eng_sync
        self.nwaits = 0
        self.nops = 0

    def _need(self, eng, reads, writes):
        need = {}

        def add(w):
            if w is None:
                return
            s, v = w
            if v is None:
                if s == eng:
                    return
                raise RuntimeError("dependency on un-flushed deferred op")
            if need.get(s, 0) < v:
                need[s] = v
        for k in reads:
            add(self.lastw.get(k))
        for k in writes:
            add(self.lastw.get(k))
            for s, v in self.readers.get(k, {}).items():
                add((s, v))
        return need

    def _waits(self, eng, need):
        e = self.E[eng]
        for s, v in need.items():
            if s == eng and (eng == 'pe' or not self.same):
                continue
            if self.seen[eng].get(s, 0) >= v:
                continue
            e.wait_ge(self.sem[s], v)
            self.seen[eng][s] = v
            self.nwaits += 1

    def _record(self, semname, val, reads, writes):
        for k in reads:
            self.readers.setdefault(k, {})[semname] = val
        for k in writes:
            self.lastw[k] = (semname, val)
            self.readers[k] = {}

    def op(self, eng, fn, reads=(), writes=(), inc=True):
        self.nops += 1
        need = self._need(eng, reads, writes)
        self._waits(eng, need)
        ins = fn(self.E[eng])
        pr, pw = self.pend[eng]
        if inc:
            self.cnt[eng] += 1
            ins.then_inc(self.sem[eng], 1)
            self._record(eng, self.cnt[eng], list(reads) + pr, list(writes) + pw)
            self.pend[eng] = ([], [])
        else:
            for k in reads:
                self.readers.setdefault(k, {})[eng] = None
            for k in writes:
                self.lastw[k] = (eng, None)
                self.readers[k] = {}
            pr.extend(reads)
            pw.extend(writes)
        return ins

    def dma(self, q, out, in_, reads=(), writes=()):
        self.nops += 1
        names = self.dq[q]
        s = names[self.dqi[q] % len(names)]
        self.dqi[q] += 1
        need = self._need(q, reads, writes)
        if self.cnt[s] > 0:
            need[s] = max(need.get(s, 0), self.cnt[s])
        self._waits(q, need)
        self.cnt[s] += 16
        ins = self.E[q].dma_start(out=out, in_=in_)
        ins.then_inc(self.sem[s], 16)
        self._record(s, self.cnt[s], reads, writes)
        return ins

    def finish(self, eng='sp'):
        need = {}
        for q, names in self.dq.items():
            for s in names:
                if self.cnt[s] > 0:
                    need[s] = self.cnt[s]
        for e in ['pe', 'dve', 'act', 'pool']:
            if self.cnt[e] > 0:
                need[e] = self.cnt[e]
        self._waits(eng, need)


def _barrier(self):
    need = {}
    for s, c in self.cnt.items():
        if c > 0:
            need[s] = c
    for e in ['pe', 'dve', 'act', 'pool', 'sp']:
        n2 = {s: v for s, v in need.items() if s != e}
        self._waits(e, n2)


Sch.barrier = _barrier


GC = 2.0 * math.sqrt(2.0 / math.pi)


def emit_lru(S, nc, es, T, io, TT=512):
    A = lambda name, shape, dt=F32: es.enter_context(nc.sbuf_tensor("lru_" + name, shape, dt))
    pp = A("pp", [128, 16]); wr = A("wr", [128, 128]); wi = A("wi", [128, 128])
    cv = A("cv", [128, 1]); h0 = A("h0", [128, 1])
    xb = [A("xb%d" % i, [128, TT + 3]) for i in range(2)]
    gt = [A("gt%d" % i, [128, TT]) for i in range(2)]
    xc = A("xc", [128, TT]); r = A("r", [128, TT]); ii = A("ii", [128, TT]); a = A("a", [128, TT])
    t1 = A("t1", [128, TT]); t2 = A("t2", [128, TT]); hh = [A("hh%d" % i, [128, TT]) for i in range(2)]
    yo = [A("yo%d" % i, [128, TT]) for i in range(2)]
    psr = es.enter_context(nc.psum_tensor("lru_psr", [128, TT], F32))
    psi = es.enter_context(nc.psum_tensor("lru_psi", [128, TT], F32))
    K = lambda *a_: ("lru",) + a_
    S.dma('sp', pp[:], io['pp'], writes=[K("pp")])
    S.dma('sp', wr[:], io['wr'], writes=[K("wr")])
    S.dma('sp', wi[:], io['wi'], writes=[K("wi")])
    S.op('act', lambda e: e.activation(out=cv[:], in_=pp[:, 7:8], func=AF.Exp, scale=-1.0), reads=[K("pp")], writes=[K("cv")])
    S.op('act', lambda e: e.activation(out=cv[:], in_=cv[:], func=AF.Ln, bias=1.0), reads=[K("cv")], writes=[K("cv")])
    S.op('dve', lambda e: e.tensor_scalar(out=cv[:], in0=cv[:], scalar1=-8.0, scalar2=None, op0=ALU.mult), reads=[K("cv")], writes=[K("cv")])
    S.op('dve', lambda e: e.memset(h0[:], 0.0), writes=[K("h0")])
    S.op('dve', lambda e: e.memset(xb[1][:, TT:TT + 3], 0.0), writes=[K("xbm", 1)])
    nt = T // TT
    for ti in range(nt):
        b = ti % 2
        pb = (ti + 1) % 2
        c0 = ti * TT
        S.dma('sp', xb[b][:, 3:TT + 3], io['lx'][:, c0:c0 + TT], writes=[K("xbm", b)])
        S.dma('sp', gt[b][:], io['lg'][:, c0:c0 + TT], writes=[K("gt", b)])
        S.op('pool', lambda e: e.tensor_copy(out=xb[b][:, 0:3], in_=xb[pb][:, TT:TT + 3]), reads=[K("xbm", pb), K("xb", pb)], writes=[K("xb", b)])
        X = [K("xb", b), K("xbm", b)]
        S.op('dve', lambda e: e.tensor_scalar(out=xc[:], in0=xb[b][:, 0:TT], scalar1=pp[:, 0:1], scalar2=pp[:, 4:5], op0=ALU.mult, op1=ALU.add), reads=X + [K("pp")], writes=[K("xc")])
        for j in (1, 2, 3):
            S.op('dve', lambda e: e.scalar_tensor_tensor(out=xc[:], in0=xb[b][:, j:j + TT], scalar=pp[:, j:j + 1], in1=xc[:], op0=ALU.mult, op1=ALU.add), reads=X + [K("pp"), K("xc")], writes=[K("xc")])
        S.op('pe', lambda e: e.matmul(psr[:], lhsT=wr[:], rhs=xc[:], start=True, stop=True), reads=[K("wr"), K("xc")], writes=[K("psr")])
        S.op('pe', lambda e: e.matmul(psi[:], lhsT=wi[:], rhs=xc[:], start=True, stop=True), reads=[K("wi"), K("xc")], writes=[K("psi")])
        S.op('act', lambda e: e.activation(out=r[:], in_=psr[:], func=AF.Sigmoid, bias=pp[:, 5:6]), reads=[K("psr"), K("pp")], writes=[K("r")])
        S.op('act', lambda e: e.activation(out=ii[:], in_=psi[:], func=AF.Sigmoid, bias=pp[:, 6:7]), reads=[K("psi"), K("pp")], writes=[K("ii")])
        S.op('act', lambda e: e.activation(out=a[:], in_=r[:], func=AF.Exp, scale=cv[:, 0:1]), reads=[K("r"), K("cv")], writes=[K("a")])
        S.op('dve', lambda e: e.tensor_tensor(out=t1[:], in0=a[:], in1=a[:], op=ALU.mult), reads=[K("a")], writes=[K("t1")])
        S.op('dve', lambda e: e.tensor_scalar(out=t1[:], in0=t1[:], scalar1=-1.0, scalar2=1.0, op0=ALU.mult, op1=ALU.add), reads=[K("t1")], writes=[K("t1")])
        S.op('act', lambda e: e.activation(out=t1[:], in_=t1[:], func=AF.Sqrt), reads=[K("t1")], writes=[K("t1")])
        S.op('dve', lambda e: e.tensor_tensor(out=t2[:], in0=ii[:], in1=xc[:], op=ALU.mult), reads=[K("ii"), K("xc")], writes=[K("t2")])
        S.op('dve', lambda e: e.tensor_tensor(out=t2[:], in0=t2[:], in1=t1[:], op=ALU.mult), reads=[K("t1"), K("t2")], writes=[K("t2")])
        init = h0[:, 0:1] if ti == 0 else hh[pb][:, TT - 1:TT]
        S.op('dve', lambda e: e.tensor_tensor_scan(out=hh[b][:], data0=a[:], data1=t2[:], initial=init, op0=ALU.mult, op1=ALU.add), reads=[K("a"), K("t2"), K("h0"), K("hh", pb)], writes=[K("hh", b)])
        S.op('pool', lambda e: e.tensor_tensor(out=t1[:], in0=gt[b][:], in1=gt[b][:], op=ALU.mult), reads=[K("gt", b)], writes=[K("t1")])
        S.op('pool', lambda e: e.tensor_scalar(out=t1[:], in0=t1[:], scalar1=0.044715, scalar2=1.0, op0=ALU.mult, op1=ALU.add), reads=[K("t1")], writes=[K("t1")])
        S.op('pool', lambda e: e.tensor_tensor(out=t1[:], in0=t1[:], in1=gt[b][:], op=ALU.mult), reads=[K("t1"), K("gt", b)], writes=[K("t1")])
        S.op('act', lambda e: e.activation(out=t1[:], in_=t1[:], func=AF.Sigmoid, scale=GC), reads=[K("t1")], writes=[K("t1")])
        S.op('pool', lambda e: e.tensor_tensor(out=t1[:], in0=t1[:], in1=gt[b][:], op=ALU.mult), reads=[K("t1"), K("gt", b)], writes=[K("t1")])
        S.op('dve', lambda e: e.tensor_tensor(out=yo[b][:], in0=hh[b][:], in1=t1[:], op=ALU.mult), reads=[K("hh", b), K("t1")], writes=[K("yo", b)])
        S.dma('sp', io['y'][:, c0:c0 + TT], yo[b][:], reads=[K("yo", b)])


EPS = 1e-6


def hg_consts():
    f = np.arange(512)
    p = np.arange(64)[:, None]
    maskc = ((f[None, :] % 64) >= p).astype(np.float32)
    reset = np.ones((128, 512), np.float32); reset[:, ::64] = 0.0
    return dict(ident=np.eye(128, dtype=np.float32), maskc=maskc, reset=reset, ones=np.ones((128, 128), np.float32))


def emit_hg(S, nc, es, T, io, TT=512):
    A = lambda name, shape, dt=F32: es.enter_context(nc.sbuf_tensor("hg_" + name, shape, dt))
    P = lambda name, shape: es.enter_context(nc.psum_tensor("hg_" + name, shape, F32))
    K = lambda *a_: ("hg",) + a_
    NCH = TT // 64
    pp = A("pp", [128, 8]); ident = A("ident", [128, 128]); ones = A("ones", [128, 128])
    maskc = A("maskc", [64, TT]); reset = A("reset", [128, TT])
    lb = A("lb", [128, 1]); oml = A("oml", [128, 1]); tmpc = A("tmpc", [128, 2])
    qin = [A("qin%d" % i, [128, TT]) for i in range(2)]
    fin = [A("fin%d" % i, [128, TT]) for i in range(2)]
    gin = [A("gin%d" % i, [128, TT]) for i in range(2)]
    vin = [A("vin%d" % i, [64, NCH, 128]) for i in range(2)]
    fv = A("fv", [128, TT]); lf = A("lf", [128, TT]); kk = A("kk", [128, TT]); cum = A("cum", [128, TT])
    dq = A("dq", [128, TT]); eq = A("eq", [128, TT]); qs = A("qs", [128, TT])
    qt = A("qt", [128, TT]); kt = A("kt", [128, TT]); qi = A("qi", [128, TT]); kh = A("kh", [128, TT])
    elast = A("elast", [128, NCH]); khT = A("khT", [64, NCH, 128]); attT = A("attT", [64, TT])
    Sst = [A("S%d" % i, [128, 128]) for i in range(2)]
    sq = A("sq", [128, TT]); rs = A("rs", [128, TT]); yo = [A("yo%d" % i, [128, TT]) for i in range(2)]
    ps_tr = P("ps_tr", [128, NCH * 128]); ps_att = P("ps_att", [128, TT]); ps_o = P("ps_o", [128, TT])
    ps_S = P("ps_S", [128, 512]); ps_ss = P("ps_ss", [128, TT])
    for nm, t_ in (("pp", pp), ("ident", ident), ("ones", ones), ("maskc", maskc), ("reset", reset)):
        S.dma('sp', t_[:], io[nm], writes=[K(nm)])
    S.op('act', lambda e: e.activation(out=tmpc[:], in_=pp[:, 0:2], func=AF.Exp), reads=[K("pp")], writes=[K("tmpc")])
    S.op('dve', lambda e: e.tensor_tensor(out=lb[:], in0=tmpc[:, 0:1], in1=tmpc[:, 1:2], op=ALU.add), reads=[K("tmpc")], writes=[K("lb")])
    S.op('dve', lambda e: e.reciprocal(out=lb[:], in_=lb[:]), reads=[K("lb")], writes=[K("lb")])
    S.op('dve', lambda e: e.tensor_tensor(out=lb[:], in0=lb[:], in1=tmpc[:, 1:2], op=ALU.mult), reads=[K("lb"), K("tmpc")], writes=[K("lb")])
    S.op('dve', lambda e: e.tensor_tensor(out=lb[:], in0=lb[:], in1=pp[:, 2:3], op=ALU.mult), reads=[K("lb"), K("pp")], writes=[K("lb")])
    S.op('dve', lambda e: e.tensor_scalar(out=oml[:], in0=lb[:], scalar1=-1.0, scalar2=1.0, op0=ALU.mult, op1=ALU.add), reads=[K("lb")], writes=[K("oml")])
    S.op('dve', lambda e: e.memset(Sst[0][:], 0.0), writes=[K("S", 0)])
    scale = 128.0 ** -0.5
    nt = T // TT
    g = 0
    for ti in range(nt):
        b = ti % 2
        c0 = ti * TT
        S.dma('sp', qin[b][:], io['q'][:, c0:c0 + TT], writes=[K("qin", b)])
        S.dma('sp', fin[b][:], io['f'][:, c0:c0 + TT], writes=[K("fin", b)])
        S.dma('sp', gin[b][:], io['g'][:, c0:c0 + TT], writes=[K("gin", b)])
        S.dma('sp', vin[b][:], io['v'][:, ti * NCH:(ti + 1) * NCH, :], writes=[K("vin", b)])
        S.op('act', lambda e: e.activation(out=fv[:], in_=fin[b][:], func=AF.Sigmoid), reads=[K("fin", b)], writes=[K("fv")])
        S.op('dve', lambda e: e.tensor_scalar(out=fv[:], in0=fv[:], scalar1=oml[:, 0:1], scalar2=lb[:, 0:1], op0=ALU.mult, op1=ALU.add), reads=[K("fv"), K("oml"), K("lb")], writes=[K("fv")])
        S.op('act', lambda e: e.activation(out=lf[:], in_=fv[:], func=AF.Ln), reads=[K("fv")], writes=[K("lf")])
        S.op('pool', lambda e: e.tensor_scalar(out=kk[:], in0=fv[:], scalar1=-1.0, scalar2=1.0, op0=ALU.mult, op1=ALU.add), reads=[K("fv")], writes=[K("kk")])
        S.op('dve', lambda e: e.tensor_tensor_scan(out=cum[:], data0=reset[:], data1=lf[:], initial=0.0, op0=ALU.mult, op1=ALU.add), reads=[K("reset"), K("lf")], writes=[K("cum")])
        cum3 = cum[:].rearrange("p (c j) -> p c j", j=64)
        v3 = lambda t_: t_[:].rearrange("p (c j) -> p c j", j=64)
        S.op('dve', lambda e: e.tensor_tensor(out=v3(dq), in0=cum3, in1=cum3[:, :, 31:32].to_broadcast([128, NCH, 64]), op=ALU.subtract), reads=[K("cum")], writes=[K("dq")])
        S.op('act', lambda e: e.activation(out=eq[:], in_=dq[:], func=AF.Exp), reads=[K("dq")], writes=[K("eq")])
        S.op('act', lambda e: e.activation(out=qs[:], in_=qin[b][:], func=AF.Silu), reads=[K("qin", b)], writes=[K("qs")])
        S.op('dve', lambda e: e.scalar_tensor_tensor(out=qt[:], in0=qs[:], scalar=scale, in1=eq[:], op0=ALU.mult, op1=ALU.mult), reads=[K("qs"), K("eq")], writes=[K("qt")])
        S.op('act', lambda e: e.activation(out=eq[:], in_=dq[:], func=AF.Exp, scale=-1.0), reads=[K("dq"), K("qt")], writes=[K("eq")])
        S.op('dve', lambda e: e.tensor_tensor(out=kt[:], in0=kk[:], in1=eq[:], op=ALU.mult), reads=[K("kk"), K("eq")], writes=[K("kt")])
        S.op('act', lambda e: e.activation(out=eq[:], in_=cum[:], func=AF.Exp), reads=[K("cum"), K("kt")], writes=[K("eq")])
        S.op('dve', lambda e: e.scalar_tensor_tensor(out=qi[:], in0=qs[:], scalar=scale, in1=eq[:], op0=ALU.mult, op1=ALU.mult), reads=[K("qs"), K("eq")], writes=[K("qi")])
        S.op('act', lambda e: e.activation(out=elast[:], in_=cum3[:, :, 63], func=AF.Exp), reads=[K("cum")], writes=[K("elast")])
        S.op('dve', lambda e: e.tensor_tensor(out=v3(dq), in0=cum3, in1=cum3[:, :, 63:64].to_broadcast([128, NCH, 64]), op=ALU.subtract), reads=[K("cum"), K("dq")], writes=[K("dq")])
        S.op('act', lambda e: e.activation(out=eq[:], in_=dq[:], func=AF.Exp, scale=-1.0), reads=[K("dq"), K("qi")], writes=[K("eq")])
        S.op('dve', lambda e: e.tensor_tensor(out=kh[:], in0=kk[:], in1=eq[:], op=ALU.mult), reads=[K("kk"), K("eq")], writes=[K("kh")])
        for n in range(NCH):
            S.op('pe', lambda e: e.transpose(out=ps_tr[0:64, n * 128:(n + 1) * 128], in_=kh[:, n * 64:(n + 1) * 64], identity=ident[:]),
                 reads=[K("kh"), K("ident")], writes=[K("ps_tr")], inc=(n == NCH - 1))
        S.op('act', lambda e: e.copy(out=khT[:].rearrange("p c d -> p (c d)"), in_=ps_tr[0:64, :]), reads=[K("ps_tr")], writes=[K("khT")])
        for n in range(NCH):
            S.op('pe', lambda e: e.matmul(ps_att[0:64, n * 64:(n + 1) * 64], lhsT=kt[:, n * 64:(n + 1) * 64], rhs=qt[:, n * 64:(n + 1) * 64], start=True, stop=True),
                 reads=[K("kt"), K("qt")], writes=[K("ps_att")], inc=(n == NCH - 1))
        S.op('dve', lambda e: e.tensor_tensor(out=attT[:], in0=ps_att[0:64, :], in1=maskc[:], op=ALU.mult), reads=[K("ps_att"), K("maskc")], writes=[K("attT")])
        for n in range(NCH):
            cs = slice(n * 64, (n + 1) * 64)
            sc, sn = g % 2, (g + 1) % 2
            S.op('pe', lambda e: e.matmul(ps_o[:, cs], lhsT=vin[b][:, n, :], rhs=attT[:, cs], start=True, stop=False),
                 reads=[K("vin", b), K("attT")], writes=[K("ps_o")], inc=False)
            S.op('pe', lambda e: e.matmul(ps_o[:, cs], lhsT=Sst[sc][:], rhs=qi[:, cs], start=False, stop=True),
                 reads=[K("S", sc), K("qi")], writes=[K("ps_o")], inc=False)
            S.op('pe', lambda e: e.matmul(ps_S[:, 0:128], lhsT=khT[:, n, :], rhs=vin[b][:, n, :], start=True, stop=True),
                 reads=[K("khT"), K("vin", b)], writes=[K("ps_S")])
            S.op('dve', lambda e: e.scalar_tensor_tensor(out=Sst[sn][:], in0=Sst[sc][:], scalar=elast[:, n:n + 1], in1=ps_S[:, 0:128], op0=ALU.mult, op1=ALU.add),
                 reads=[K("S", sc), K("elast"), K("ps_S")], writes=[K("S", sn)])
            g += 1
        S.op('act', lambda e: e.activation(out=sq[:], in_=ps_o[:], func=AF.Square), reads=[K("ps_o")], writes=[K("sq")])
        S.op('pe', lambda e: e.matmul(ps_ss[:], lhsT=ones[:], rhs=sq[:], start=True, stop=True), reads=[K("ones"), K("sq")], writes=[K("ps_ss")])
        S.op('dve', lambda e: e.tensor_scalar(out=rs[:], in0=ps_ss[:], scalar1=1.0 / 128.0, scalar2=EPS, op0=ALU.mult, op1=ALU.add), reads=[K("ps_ss")], writes=[K("rs")])
        S.op('act', lambda e: e.activation(out=rs[:], in_=rs[:], func=AF.Sqrt), reads=[K("rs")], writes=[K("rs")])
        S.op('dve', lambda e: e.reciprocal(out=rs[:], in_=rs[:]), reads=[K("rs")], writes=[K("rs")])
        S.op('dve', lambda e: e.scalar_tensor_tensor(out=rs[:], in0=ps_o[:], scalar=pp[:, 3:4], in1=rs[:], op0=ALU.mult, op1=ALU.mult), reads=[K("ps_o"), K("pp"), K("rs")], writes=[K("rs")])
        S.op('act', lambda e: e.activation(out=sq[:], in_=gin[b][:], func=AF.Silu), reads=[K("gin", b), K("sq")], writes=[K("sq")])
        S.op('dve', lambda e: e.tensor_tensor(out=yo[b][:], in0=rs[:], in1=sq[:], op=ALU.mult), reads=[K("rs"), K("sq")], writes=[K("yo", b)])
        S.dma('sp', io['y'][:, c0:c0 + TT], yo[b][:], reads=[K("yo", b)])


EPS = 1e-6


def ret_consts(head, T):
    gamma = 1.0 - 2.0 ** (-5.0 - head)
    lg = np.log(np.float32(gamma)).astype(np.float32)
    scale = 256.0 ** -0.5
    f = np.arange(512) % 64
    qdec = np.tile((np.exp(lg * f) * scale)[None, :], (128, 1)).astype(np.float32)
    kdec = np.tile(np.exp(lg * (64 - f))[None, :], (128, 1)).astype(np.float32)
    p = np.arange(64)[:, None]
    dmask = (np.exp(lg * np.abs(f[None, :] - p)) * scale).astype(np.float32)
    half = 128
    inv_freq = 1.0 / (10000.0 ** (np.arange(half, dtype=np.float32) / half))
    ang = np.arange(T, dtype=np.float32)[None, :] * inv_freq[:, None].astype(np.float32)
    g64 = np.full((128,), np.exp(lg * 64), np.float32)
    return dict(qdec=qdec, kdec=kdec, dmask=dmask, cos=np.cos(ang).astype(np.float32), sin=np.sin(ang).astype(np.float32),
                ident=np.eye(128, dtype=np.float32), ones=np.ones((128, 128), np.float32)), g64


def emit_ret(S, nc, es, T, io, TT=512):
    A = lambda name, shape, dt=F32: es.enter_context(nc.sbuf_tensor("rt_" + name, shape, dt))
    P = lambda name, shape: es.enter_context(nc.psum_tensor("rt_" + name, shape, F32))
    K = lambda *a_: ("rt",) + a_
    NCH = TT // 64
    pp = A("pp", [128, 4]); ident = A("ident", [128, 128]); ones = A("ones", [128, 128])
    qdec = A("qdec", [128, TT]); kdec = A("kdec", [128, TT]); dmask = A("dmask", [64, TT])
    qin = [A("qin%d" % i, [128, 2, TT]) for i in range(2)]
    kin = [A("kin%d" % i, [128, 2, TT]) for i in range(2)]
    gin = [A("gin%d" % i, [128, 2, TT]) for i in range(2)]
    vin = [A("vin%d" % i, [64, NCH, 256]) for i in range(2)]
    cs_ = [A("cos%d" % i, [128, TT]) for i in range(2)]
    sn_ = [A("sin%d" % i, [128, TT]) for i in range(2)]
    ta = A("ta", [128, TT]); tb = A("tb", [128, TT])
    qr = A("qr", [128, 2, TT]); kr = A("kr", [128, 2, TT]); qi = A("qi", [128, 2, TT]); kh = A("kh", [128, 2, TT])
    khT = A("khT", [64, NCH, 256]); attT = A("attT", [64, TT])
    Sst = [A("S%d" % i, [128, 2, 256]) for i in range(2)]
    osb = A("osb", [128, 2, TT]); sq = A("sq", [128, 2, TT]); mean = A("mean", [128, TT]); rs = A("rs", [128, TT])
    yo = [A("yo%d" % i, [128, 2, TT]) for i in range(2)]
    ps_tr = P("ps_tr", [128, 1024]); ps_att = P("ps_att", [128, TT]); ps_o = [P("ps_o%d" % i, [128, TT]) for i in range(2)]
    ps_S = P("ps_S", [128, 512]); ps_sum = P("ps_sum", [128, TT]); ps_sq = P("ps_sq", [128, TT])
    for nm, t_ in (("pp", pp), ("ident", ident), ("ones", ones), ("qdec", qdec), ("kdec", kdec), ("dmask", dmask)):
        S.dma('sp', t_[:], io[nm], writes=[K(nm)])
    S.op('dve', lambda e: e.memset(Sst[0][:], 0.0), writes=[K("S", 0)])
    nt = T // TT
    g = 0
    for ti in range(nt):
        b = ti % 2
        c0 = ti * TT
        S.dma('sp', qin[b][:], io['q'][:, :, c0:c0 + TT], writes=[K("qin", b)])
        S.dma('sp', kin[b][:], io['k'][:, :, c0:c0 + TT], writes=[K("kin", b)])
        S.dma('sp', gin[b][:], io['g'][:, :, c0:c0 + TT], writes=[K("gin", b)])
        S.dma('sp', vin[b][:], io['v'][:, ti * NCH:(ti + 1) * NCH, :], writes=[K("vin", b)])
        S.dma('sp', cs_[b][:], io['cos'][:, c0:c0 + TT], writes=[K("cos", b)])
        S.dma('sp', sn_[b][:], io['sin'][:, c0:c0 + TT], writes=[K("sin", b)])
        for (src, skey, dst, dkey) in ((qin[b], K("qin", b), qr, K("qr")), (kin[b], K("kin", b), kr, K("kr"))):
            C_, S_ = [K("cos", b)], [K("sin", b)]
            S.op('dve', lambda e: e.tensor_tensor(out=ta[:], in0=src[:, 0, :], in1=cs_[b][:], op=ALU.mult), reads=[skey] + C_, writes=[K("ta")])
            S.op('pool', lambda e: e.tensor_tensor(out=tb[:], in0=src[:, 1, :], in1=sn_[b][:], op=ALU.mult), reads=[skey] + S_, writes=[K("tb")])
            S.op('dve', lambda e: e.tensor_tensor(out=dst[:, 0, :], in0=ta[:], in1=tb[:], op=ALU.subtract), reads=[K("ta"), K("tb")], writes=[dkey])
            S.op('dve', lambda e: e.tensor_tensor(out=ta[:], in0=src[:, 0, :], in1=sn_[b][:], op=ALU.mult), reads=[skey] + S_, writes=[K("ta")])
            S.op('pool', lambda e: e.tensor_tensor(out=tb[:], in0=src[:, 1, :], in1=cs_[b][:], op=ALU.mult), reads=[skey] + C_, writes=[K("tb")])
            S.op('dve', lambda e: e.tensor_tensor(out=dst[:, 1, :], in0=ta[:], in1=tb[:], op=ALU.add), reads=[K("ta"), K("tb")], writes=[dkey])
        S.op('pool', lambda e: e.tensor_tensor(out=qi[:], in0=qr[:], in1=qdec[:].unsqueeze(1).to_broadcast([128, 2, TT]), op=ALU.mult), reads=[K("qr"), K("qdec")], writes=[K("qi")])
        S.op('dve', lambda e: e.tensor_tensor(out=kh[:], in0=kr[:], in1=kdec[:].unsqueeze(1).to_broadcast([128, 2, TT]), op=ALU.mult), reads=[K("kr"), K("kdec")], writes=[K("kh")])
        for hf in range(2):
            for n4 in range(4):
                n = hf * 4 + n4
                for dt in range(2):
                    S.op('pe', lambda e: e.transpose(out=ps_tr[0:64, n4 * 256 + dt * 128:n4 * 256 + (dt + 1) * 128], in_=kh[:, dt, n * 64:(n + 1) * 64], identity=ident[:]),
                         reads=[K("kh"), K("ident")], writes=[K("ps_tr")], inc=(n4 == 3 and dt == 1))
            S.op('act', lambda e: e.copy(out=khT[:, hf * 4:(hf + 1) * 4, :].rearrange("p c d -> p (c d)"), in_=ps_tr[0:64, :]), reads=[K("ps_tr")], writes=[K("khT")])
        for n in range(NCH):
            cs = slice(n * 64, (n + 1) * 64)
            S.op('pe', lambda e: e.matmul(ps_att[0:64, cs], lhsT=kr[:, 0, cs], rhs=qr[:, 0, cs], start=True, stop=False), reads=[K("kr"), K("qr")], writes=[K("ps_att")], inc=False)
            S.op('pe', lambda e: e.matmul(ps_att[0:64, cs], lhsT=kr[:, 1, cs], rhs=qr[:, 1, cs], start=False, stop=True), reads=[K("kr"), K("qr")], writes=[K("ps_att")], inc=(n == NCH - 1))
        S.op('dve', lambda e: e.tensor_tensor(out=attT[:], in0=ps_att[0:64, :], in1=dmask[:], op=ALU.mult), reads=[K("ps_att"), K("dmask")], writes=[K("attT")])
        for n in range(NCH):
            cs = slice(n * 64, (n + 1) * 64)
            sc, sn = g % 2, (g + 1) % 2
            for et in range(2):
                es_ = slice(et * 128, (et + 1) * 128)
                S.op('pe', lambda e: e.matmul(ps_o[et][:, cs], lhsT=vin[b][:, n, es_], rhs=attT[:, cs], start=True, stop=False), reads=[K("vin", b), K("attT")], writes=[K("ps_o", et)], inc=False)
                S.op('pe', lambda e: e.matmul(ps_o[et][:, cs], lhsT=Sst[sc][:, 0, es_], rhs=qi[:, 0, cs], start=False, stop=False), reads=[K("S", sc), K("qi")], writes=[K("ps_o", et)], inc=False)
                S.op('pe', lambda e: e.matmul(ps_o[et][:, cs], lhsT=Sst[sc][:, 1, es_], rhs=qi[:, 1, cs], start=False, stop=True), reads=[K("S", sc), K("qi")], writes=[K("ps_o", et)], inc=False)
            for dt in range(2):
                S.op('pe', lambda e: e.matmul(ps_S[:, dt * 256:(dt + 1) * 256], lhsT=khT[:, n, dt * 128:(dt + 1) * 128], rhs=vin[b][:, n, :], start=True, stop=True),
                     reads=[K("khT"), K("vin", b)], writes=[K("ps_S")], inc=(dt == 1))
            S.op('dve', lambda e: e.scalar_tensor_tensor(out=Sst[sn][:].rearrange("p a b -> p (a b)"), in0=Sst[sc][:].rearrange("p a b -> p (a b)"), scalar=pp[:, 0:1], in1=ps_S[:], op0=ALU.mult, op1=ALU.add),
                 reads=[K("S", sc), K("pp"), K("ps_S")], writes=[K("S", sn)])
            g += 1
        for et in range(2):
            S.op('act', lambda e: e.copy(out=osb[:, et, :], in_=ps_o[et][:]), reads=[K("ps_o", et)], writes=[K("osb", et)])
            S.op('act', lambda e: e.activation(out=sq[:, et, :], in_=ps_o[et][:], func=AF.Square), reads=[K("ps_o", et)], writes=[K("sq", et)])
        for et in range(2):
            S.op('pe', lambda e: e.matmul(ps_sum[:], lhsT=ones[:], rhs=osb[:, et, :], start=(et == 0), stop=(et == 1)), reads=[K("ones"), K("osb", et)], writes=[K("ps_sum")], inc=(et == 1))
        for et in range(2):
            S.op('pe', lambda e: e.matmul(ps_sq[:], lhsT=ones[:], rhs=sq[:, et, :], start=(et == 0), stop=(et == 1)), reads=[K("ones"), K("sq", et)], writes=[K("ps_sq")], inc=(et == 1))
        S.op('dve', lambda e: e.tensor_scalar(out=mean[:], in0=ps_sum[:], scalar1=1.0 / 256.0, scalar2=None, op0=ALU.mult), reads=[K("ps_sum")], writes=[K("mean")])
        S.op('dve', lambda e: e.tensor_tensor(out=rs[:], in0=mean[:], in1=mean[:], op=ALU.mult), reads=[K("mean")], writes=[K("rs")])
        S.op('dve', lambda e: e.scalar_tensor_tensor(out=rs[:], in0=ps_sq[:], scalar=1.0 / 256.0, in1=rs[:], op0=ALU.mult, op1=ALU.subtract), reads=[K("ps_sq"), K("rs")], writes=[K("rs")])
        S.op('dve', lambda e: e.tensor_scalar(out=rs[:], in0=rs[:], scalar1=EPS, scalar2=None, op0=ALU.add), reads=[K("rs")], writes=[K("rs")])
        S.op('act', lambda e: e.activation(out=rs[:], in_=rs[:], func=AF.Sqrt), reads=[K("rs")], writes=[K("rs")])
        S.op('dve', lambda e: e.reciprocal(out=rs[:], in_=rs[:]), reads=[K("rs")], writes=[K("rs")])
        for et in range(2):
            S.op('dve', lambda e: e.tensor_tensor(out=osb[:, et, :], in0=osb[:, et, :], in1=mean[:], op=ALU.subtract), reads=[K("osb", et), K("mean"), K("ps_sum")], writes=[K("osb", et)])
            S.op('dve', lambda e: e.scalar_tensor_tensor(out=osb[:, et, :], in0=osb[:, et, :], scalar=pp[:, 1 + et:2 + et], in1=rs[:], op0=ALU.mult, op1=ALU.mult), reads=[K("osb", et), K("pp"), K("rs")], writes=[K("osb", et)])
            S.op('act', lambda e: e.activation(out=sq[:, et, :], in_=gin[b][:, et, :], func=AF.Silu), reads=[K("gin", b), K("sq", et), K("ps_sq")], writes=[K("sq", et)])
            S.op('dve', lambda e: e.tensor_tensor(out=yo[b][:, et, :], in0=osb[:, et, :], in1=sq[:, et, :], op=ALU.mult), reads=[K("osb", et), K("sq", et)], writes=[K("yo", b)])
        S.dma('sp', io['y'][:, :, c0:c0 + TT], yo[b][:], reads=[K("yo", b)])


EPS = 1e-6
NEG = -1.0e30


def dn_consts():
    f = np.arange(512) % 64
    p = np.arange(64)[:, None]
    cneg = np.where(f[None, :] >= p, 0.0, NEG).astype(np.float32)
    lneg = np.where(p > f[None, :], 0.0, NEG).astype(np.float32)
    ustrict = (f[None, :] > p).astype(np.float32)
    tri = (np.arange(64)[:, None] <= np.arange(64)[None, :]).astype(np.float32)
    eye8 = (f[None, :] == p).astype(np.float32)
    reset = np.ones((128, 512), np.float32); reset[:, ::64] = 0.0
    return dict(ident=np.eye(128, dtype=np.float32), ones=np.ones((128, 128), np.float32), cneg=cneg, lneg=lneg,
                ustrict=ustrict, tri=tri, eye8=eye8, reset=reset)


def emit_dn(S, nc, es, T, io, TT=512):
    A = lambda name, shape, dt=F32: es.enter_context(nc.sbuf_tensor("dn_" + name, shape, dt))
    K = lambda *a_: ("dn",) + a_
    NCH = TT // 64
    NC_ = T // 64
    bank = [es.enter_context(nc.psum_tensor("dn_bank%d" % i, [128, 512], F32)) for i in range(8)]
    BK = lambda i: K("bank", i)
    pp = A("pp", [128, 16]); ident = A("ident", [128, 128]); ones = A("ones", [128, 128])
    cneg = A("cneg", [64, TT]); lneg = A("lneg", [64, TT]); ustrict = A("ustrict", [64, TT]); tri = A("tri", [64, 64])
    eye8 = A("eye8", [64, TT]); reset = A("reset", [128, TT])
    abt = A("abt", [64, 2, NC_]); lat = A("lat", [64, NC_]); gt_ = A("gt", [64, NC_]); glast = A("glast", [128, NC_])
    betat = A("betat", [64, NC_]); bkt = A("bkt", [64, NC_]); tailt = A("tailt", [64, NC_]); cd = A("cd", [128, NC_]); Aexp = A("Aexp", [128, 1])
    for nm, t_ in (("pp", pp), ("ident", ident), ("ones", ones), ("cneg", cneg), ("lneg", lneg), ("ustrict", ustrict), ("tri", tri), ("eye8", eye8), ("reset", reset), ("ab_t", abt)):
        S.dma('sp', t_[:], io[nm], writes=[K(nm)])
    S.op('act', lambda e: e.activation(out=Aexp[:], in_=pp[:, 13:14], func=AF.Exp), reads=[K("pp")], writes=[K("Aexp")])
    S.op('act', lambda e: e.activation(out=lat[:], in_=abt[:, 0, :], func=AF.Exp, bias=pp[0:64, 14:15]), reads=[K("ab_t"), K("pp")], writes=[K("lat")])
    S.op('act', lambda e: e.activation(out=lat[:], in_=lat[:], func=AF.Ln, bias=1.0), reads=[K("lat")], writes=[K("lat")])
    S.op('dve', lambda e: e.tensor_scalar(out=lat[:], in0=lat[:], scalar1=Aexp[0:64, 0:1], scalar2=-1.0, op0=ALU.mult, op1=ALU.mult), reads=[K("lat"), K("Aexp")], writes=[K("lat")])
    S.op('pe', lambda e: e.matmul(bank[0][0:64, 0:NC_], lhsT=tri[:], rhs=lat[:], start=True, stop=True), reads=[K("tri"), K("lat")], writes=[BK(0)])
    S.op('pe', lambda e: e.matmul(bank[1][:, 0:NC_], lhsT=ones[0:64, :], rhs=lat[:], start=True, stop=True), reads=[K("ones"), K("lat")], writes=[BK(1)])
    S.op('act', lambda e: e.copy(out=gt_[:], in_=bank[0][0:64, 0:NC_]), reads=[BK(0)], writes=[K("gt")])
    S.op('act', lambda e: e.copy(out=glast[:], in_=bank[1][:, 0:NC_]), reads=[BK(1)], writes=[K("glast")])
    S.op('act', lambda e: e.activation(out=betat[:], in_=abt[:, 1, :], func=AF.Sigmoid), reads=[K("ab_t")], writes=[K("betat")])
    S.op('act', lambda e: e.activation(out=bkt[:], in_=gt_[:], func=AF.Exp), reads=[K("gt")], writes=[K("bkt")])
    S.op('dve', lambda e: e.tensor_tensor(out=bkt[:], in0=bkt[:], in1=betat[:], op=ALU.mult), reads=[K("bkt"), K("betat")], writes=[K("bkt")])
    S.op('dve', lambda e: e.tensor_tensor(out=tailt[:], in0=glast[0:64, :], in1=gt_[:], op=ALU.subtract), reads=[K("glast"), K("gt")], writes=[K("tailt")])
    S.op('act', lambda e: e.activation(out=tailt[:], in_=tailt[:], func=AF.Exp), reads=[K("tailt")], writes=[K("tailt")])
    S.op('act', lambda e: e.activation(out=cd[:], in_=glast[:], func=AF.Exp), reads=[K("glast")], writes=[K("cd")])
    xb = {w: [A("xb%s%d" % (w, i), [128, TT + 3]) for i in range(2)] for w in "qkv"}
    zin = [A("zin%d" % i, [128, TT]) for i in range(2)]
    abb = [A("abb%d" % i, [128, 2, TT]) for i in range(2)]
    cq = A("cq", [128, TT]); ck = A("ck", [128, TT]); cvv = A("cv", [128, TT])
    t1 = A("t1", [128, TT]); t2 = A("t2", [128, TT])
    qn = A("qn", [128, TT]); kn = A("kn", [128, TT]); qg = A("qg", [128, TT])
    gb = A("gb", [128, TT]); betab = A("betab", [128, TT])
    X = A("X", [64, TT]); decT = A("decT", [64, TT]); decL = A("decL", [64, TT]); bs = A("bs", [64, TT])
    Pm = [A("P%d" % i, [64, TT]) for i in range(2)]; PTm = [A("PT%d" % i, [64, TT]) for i in range(2)]
    Rm = A("R", [64, TT]); RTm = A("RT", [64, TT]); qkT = A("qkT", [64, TT])
    knT = A("knT", [64, NCH, 128]); vT = A("vT", [64, NCH, 128]); kb = A("kb", [64, NCH, 128]); ktl = A("ktl", [64, NCH, 128])
    usb = A("usb", [64, NCH, 128]); wsb = A("wsb", [128, TT]); vnew = [A("vnew%d" % i, [64, 128]) for i in range(2)]
    Sst = [A("S%d" % i, [128, 128]) for i in range(2)]
    yo = [A("yo%d" % i, [128, TT]) for i in range(2)]
    S.op('dve', lambda e: e.memset(Sst[0][:], 0.0), writes=[K("S", 0)])
    for w in "qkv":
        S.op('dve', lambda e: e.memset(xb[w][1][:, TT:TT + 3], 0.0), writes=[K("xbm" + w, 1)])
    scale = 128.0 ** -0.5
    nt = T // TT
    g = 0
    v3 = lambda ap: ap.rearrange("p (c j) -> p c j", j=64)
    for ti in range(nt):
        b = ti % 2
        pb = (ti + 1) % 2
        c0 = ti * TT
        n0 = ti * NCH
        for w in "qkv":
            S.dma('sp', xb[w][b][:, 3:TT + 3], io[w][:, c0:c0 + TT], writes=[K("xbm" + w, b)])
        S.dma('sp', zin[b][:], io['z'][:, c0:c0 + TT], writes=[K("zin", b)])
        S.dma('sp', abb[b][:], io['ab_b'][:, :, c0:c0 + TT], writes=[K("abb", b)])
        for wi_, (w, dst) in enumerate((("q", cq), ("k", ck), ("v", cvv))):
            eng = 'dve' if wi_ != 1 else 'pool'
            S.op('pool', lambda e: e.tensor_copy(out=xb[w][b][:, 0:3], in_=xb[w][pb][:, TT:TT + 3]), reads=[K("xbm" + w, pb), K("xb" + w, pb)], writes=[K("xb" + w, b)])
            XK = [K("xb" + w, b), K("xbm" + w, b), K("pp")]
            dk_ = K("c" + w)
            S.op(eng, lambda e: e.tensor_scalar(out=dst[:], in0=xb[w][b][:, 0:TT], scalar1=pp[:, 4 * wi_:4 * wi_ + 1], scalar2=None, op0=ALU.mult), reads=XK, writes=[dk_])
            for j in (1, 2, 3):
                if eng == 'dve':
                    S.op('dve', lambda e: e.scalar_tensor_tensor(out=dst[:], in0=xb[w][b][:, j:j + TT], scalar=pp[:, 4 * wi_ + j:4 * wi_ + j + 1], in1=dst[:], op0=ALU.mult, op1=ALU.add), reads=XK + [dk_], writes=[dk_])
                else:
                    S.op('pool', lambda e: e.tensor_scalar(out=t2[:], in0=xb[w][b][:, j:j + TT], scalar1=pp[:, 4 * wi_ + j:4 * wi_ + j + 1], scalar2=None, op0=ALU.mult), reads=XK, writes=[K("t2")])
                    S.op('pool', lambda e: e.tensor_tensor(out=dst[:], in0=dst[:], in1=t2[:], op=ALU.add), reads=[K("t2"), dk_], writes=[dk_])
            S.op('act', lambda e: e.activation(out=dst[:], in_=dst[:], func=AF.Silu), reads=[dk_], writes=[dk_])
        for (src, sk, dst, dk2, sc_) in ((cq, K("cq"), qn, K("qn"), scale), (ck, K("ck"), kn, K("kn"), 1.0)):
            S.op('act', lambda e: e.activation(out=t1[:], in_=src[:], func=AF.Square), reads=[sk], writes=[K("t1")])
            S.op('pe', lambda e: e.matmul(bank[5][:], lhsT=ones[:], rhs=t1[:], start=True, stop=True), reads=[K("ones"), K("t1")], writes=[BK(5)])
            S.op('dve', lambda e: e.tensor_scalar(out=t1[:], in0=bank[5][:], scalar1=EPS, scalar2=None, op0=ALU.add), reads=[BK(5)], writes=[K("t1")])
            S.op('act', lambda e: e.activation(out=t1[:], in_=t1[:], func=AF.Sqrt), reads=[K("t1")], writes=[K("t1")])
            S.op('dve', lambda e: e.reciprocal(out=t1[:], in_=t1[:]), reads=[K("t1")], writes=[K("t1")])
            S.op('dve', lambda e: e.scalar_tensor_tensor(out=dst[:], in0=src[:], scalar=sc_, in1=t1[:], op0=ALU.mult, op1=ALU.mult), reads=[sk, K("t1")], writes=[dk2])
        S.op('act', lambda e: e.activation(out=gb[:], in_=abb[b][:, 0, :], func=AF.Exp, bias=pp[:, 14:15]), reads=[K("abb", b), K("pp")], writes=[K("gb")])
        S.op('act', lambda e: e.activation(out=gb[:], in_=gb[:], func=AF.Ln, bias=1.0), reads=[K("gb")], writes=[K("gb")])
        S.op('dve', lambda e: e.tensor_scalar(out=gb[:], in0=gb[:], scalar1=Aexp[:, 0:1], scalar2=-1.0, op0=ALU.mult, op1=ALU.mult), reads=[K("gb"), K("Aexp")], writes=[K("gb")])
        S.op('dve', lambda e: e.tensor_tensor_scan(out=gb[:], data0=reset[:], data1=gb[:], initial=0.0, op0=ALU.mult, op1=ALU.add), reads=[K("reset"), K("gb")], writes=[K("gb")])
        S.op('act', lambda e: e.activation(out=betab[:], in_=abb[b][:, 1, :], func=AF.Sigmoid), reads=[K("abb", b)], writes=[K("betab")])
        S.op('act', lambda e: e.activation(out=t1[:], in_=gb[:], func=AF.Exp), reads=[K("gb")], writes=[K("t1")])
        S.op('dve', lambda e: e.tensor_tensor(out=qg[:], in0=qn[:], in1=t1[:], op=ALU.mult), reads=[K("qn"), K("t1")], writes=[K("qg")])
        S.op('dve', lambda e: e.tensor_tensor(out=v3(X[:]), in0=v3(gb[0:64, :]), in1=gt_[:, n0:n0 + NCH].unsqueeze(2).to_broadcast([64, NCH, 64]), op=ALU.subtract), reads=[K("gb"), K("gt")], writes=[K("X")])
        S.op('pool', lambda e: e.tensor_tensor(out=decT[:], in0=X[:], in1=cneg[:], op=ALU.add), reads=[K("X"), K("cneg")], writes=[K("decT")])
        S.op('act', lambda e: e.activation(out=decT[:], in_=decT[:], func=AF.Exp), reads=[K("decT")], writes=[K("decT")])
        S.op('pool', lambda e: e.tensor_tensor(out=decL[:], in0=X[:], in1=lneg[:], op=ALU.subtract), reads=[K("X"), K("lneg")], writes=[K("decL")])
        S.op('act', lambda e: e.activation(out=decL[:], in_=decL[:], func=AF.Exp, scale=-1.0), reads=[K("decL")], writes=[K("decL")])
        S.op('pool', lambda e: e.tensor_tensor(out=bs[:], in0=betab[0:64, :], in1=ustrict[:], op=ALU.mult), reads=[K("betab"), K("ustrict")], writes=[K("bs")])
        for n in range(NCH):
            cs = slice(n * 64, (n + 1) * 64)
            S.op('pe', lambda e: e.matmul(bank[1][0:64, cs], lhsT=kn[:, cs], rhs=kn[:, cs], start=True, stop=True), reads=[K("kn")], writes=[BK(1)], inc=(n == NCH - 1))
        for n in range(NCH):
            cs = slice(n * 64, (n + 1) * 64)
            S.op('pe', lambda e: e.matmul(bank[2][0:64, cs], lhsT=kn[:, cs], rhs=qn[:, cs], start=True, stop=True), reads=[K("kn"), K("qn")], writes=[BK(2)], inc=(n == NCH - 1))
        P, PT = Pm[0], PTm[0]
        S.op('dve', lambda e: e.tensor_tensor(out=P[:], in0=bank[1][0:64, :], in1=decT[:], op=ALU.mult), reads=[BK(1), K("decT")], writes=[K("P", 0)])
        S.op('dve', lambda e: e.tensor_tensor(out=P[:], in0=P[:], in1=bs[:], op=ALU.mult), reads=[K("P", 0), K("bs")], writes=[K("P", 0)])
        S.op('dve', lambda e: e.tensor_tensor(out=PT[:], in0=bank[1][0:64, :], in1=decL[:], op=ALU.mult), reads=[BK(1), K("decL")], writes=[K("PT", 0)])
        S.op('dve', lambda e: e.tensor_tensor(out=v3(PT[:]), in0=v3(PT[:]), in1=betat[:, n0:n0 + NCH].unsqueeze(2).to_broadcast([64, NCH, 64]), op=ALU.mult), reads=[K("PT", 0), K("betat")], writes=[K("PT", 0)])
        S.op('dve', lambda e: e.tensor_tensor(out=qkT[:], in0=bank[2][0:64, :], in1=decT[:], op=ALU.mult), reads=[BK(2), K("decT")], writes=[K("qkT")])
        S.op('pool', lambda e: e.tensor_tensor(out=Rm[:], in0=eye8[:], in1=P[:], op=ALU.subtract), reads=[K("eye8"), K("P", 0)], writes=[K("R")])
        S.op('pool', lambda e: e.tensor_tensor(out=RTm[:], in0=eye8[:], in1=PT[:], op=ALU.subtract), reads=[K("eye8"), K("PT", 0)], writes=[K("RT")])
        cur = 0
        for lvl in range(5):
            last = (lvl == 4)
            nx = 1 - cur
            for n in range(NCH):
                cs = slice(n * 64, (n + 1) * 64)
                S.op('pe', lambda e: e.matmul(bank[1][0:64, cs], lhsT=PTm[cur][:, cs], rhs=Pm[cur][:, cs], start=True, stop=True), reads=[K("PT", cur), K("P", cur)], writes=[BK(1)], inc=(n == NCH - 1))
            if not last:
                for n in range(NCH):
                    cs = slice(n * 64, (n + 1) * 64)
                    S.op('pe', lambda e: e.matmul(bank[2][0:64, cs], lhsT=Pm[cur][:, cs], rhs=PTm[cur][:, cs], start=True, stop=True), reads=[K("PT", cur), K("P", cur)], writes=[BK(2)], inc=(n == NCH - 1))
            S.op('act', lambda e: e.copy(out=Pm[nx][:], in_=bank[1][0:64, :]), reads=[BK(1)], writes=[K("P", nx)])
            if not last:
                S.op('dve', lambda e: e.tensor_copy(out=PTm[nx][:], in_=bank[2][0:64, :]), reads=[BK(2)], writes=[K("PT", nx)])
            for n in range(NCH):
                cs = slice(n * 64, (n + 1) * 64)
                S.op('pe', lambda e: e.matmul(bank[3][0:64, cs], lhsT=RTm[:, cs], rhs=Pm[nx][:, cs], start=True, stop=True), reads=[K("RT"), K("P", nx)], writes=[BK(3)], inc=(n == NCH - 1))
            if not last:
                for n in range(NCH):
                    cs = slice(n * 64, (n + 1) * 64)
                    S.op('pe', lambda e: e.matmul(bank[4][0:64, cs], lhsT=Pm[nx][:, cs], rhs=RTm[:, cs], start=True, stop=True), reads=[K("RT"), K("P", nx)], writes=[BK(4)], inc=(n == NCH - 1))
            S.op('dve', lambda e: e.tensor_tensor(out=Rm[:], in0=Rm[:], in1=bank[3][0:64, :], op=ALU.add), reads=[K("R"), BK(3)], writes=[K("R")])
            if not last:
                S.op('dve', lambda e: e.tensor_tensor(out=RTm[:], in0=RTm[:], in1=bank[4][0:64, :], op=ALU.add), reads=[K("RT"), BK(4)], writes=[K("RT")])
            cur = nx
        for (src, sk, dst, dk2) in ((kn, K("kn"), knT, K("knT")), (cvv, K("cv"), vT, K("vT"))):
            for hf in range(2):
                for n4 in range(4):
                    n = hf * 4 + n4
                    S.op('pe', lambda e: e.transpose(out=bank[0][0:64, n4 * 128:(n4 + 1) * 128], in_=src[:, n * 64:(n + 1) * 64], identity=ident[:]), reads=[sk, K("ident")], writes=[BK(0)], inc=(n4 == 3))
                S.op('act', lambda e: e.copy(out=dst[:, hf * 4:(hf + 1) * 4, :].rearrange("p c d -> p (c d)"), in_=bank[0][0:64, :]), reads=[BK(0)], writes=[dk2])
        bc = lambda t_: t_[:, n0:n0 + NCH].unsqueeze(2).to_broadcast([64, NCH, 128])
        S.op('pool', lambda e: e.tensor_tensor(out=kb[:], in0=knT[:], in1=bc(bkt), op=ALU.mult), reads=[K("knT"), K("bkt")], writes=[K("kb")])
        S.op('pool', lambda e: e.tensor_tensor(out=ktl[:], in0=knT[:], in1=bc(tailt), op=ALU.mult), reads=[K("knT"), K("tailt")], writes=[K("ktl")])
        S.op('dve', lambda e: e.tensor_tensor(out=vT[:], in0=vT[:], in1=bc(betat), op=ALU.mult), reads=[K("vT"), K("betat")], writes=[K("vT")])
        for hf in range(2):
            for n4 in range(4):
                n = hf * 4 + n4
                cs = slice(n * 64, (n + 1) * 64)
                S.op('pe', lambda e: e.matmul(bank[1 + hf][0:64, n4 * 128:(n4 + 1) * 128], lhsT=Rm[:, cs], rhs=vT[:, n, :], start=True, stop=True), reads=[K("R"), K("vT")], writes=[BK(1 + hf)], inc=(n4 == 3))
            S.op('act', lambda e: e.copy(out=usb[:, hf * 4:(hf + 1) * 4, :].rearrange("p c d -> p (c d)"), in_=bank[1 + hf][0:64, :]), reads=[BK(1 + hf)], writes=[K("usb")])
        for n in range(NCH):
            cs = slice(n * 64, (n + 1) * 64)
            S.op('pe', lambda e: e.matmul(bank[0][:, cs], lhsT=kb[:, n, :], rhs=Rm[:, cs], start=True, stop=True), reads=[K("kb"), K("R")], writes=[BK(0)], inc=(n == NCH - 1))
        S.op('act', lambda e: e.copy(out=wsb[:], in_=bank[0][:]), reads=[BK(0)], writes=[K("wsb")])
        for n in range(NCH):
            cs = slice(n * 64, (n + 1) * 64)
            sc, sn = g % 2, (g + 1) % 2
            vb_ = g % 2
            S.op('pe', lambda e: e.matmul(bank[7][0:64, 0:128], lhsT=wsb[:, cs], rhs=Sst[sc][:], start=True, stop=True), reads=[K("wsb"), K("S", sc)], writes=[K("ps_ws")])
            S.op('dve', lambda e: e.tensor_tensor(out=vnew[vb_][:], in0=usb[:, n, :], in1=bank[7][0:64, 0:128], op=ALU.subtract), reads=[K("usb"), K("ps_ws")], writes=[K("vnew", vb_)])
            S.op('pe', lambda e: e.matmul(bank[6][:, cs], lhsT=Sst[sc][:], rhs=qg[:, cs], start=True, stop=False), reads=[K("S", sc), K("qg")], writes=[BK(6)], inc=False)
            S.op('pe', lambda e: e.matmul(bank[6][:, cs], lhsT=vnew[vb_][:], rhs=qkT[:, cs], start=False, stop=True), reads=[K("vnew", vb_), K("qkT")], writes=[BK(6)], inc=False)
            S.op('pe', lambda e: e.matmul(bank[7][:, 128:256], lhsT=ktl[:, n, :], rhs=vnew[vb_][:], start=True, stop=True), reads=[K("ktl"), K("vnew", vb_)], writes=[K("ps_S")])
            S.op('dve', lambda e: e.scalar_tensor_tensor(out=Sst[sn][:], in0=Sst[sc][:], scalar=cd[:, n0 + n:n0 + n + 1], in1=bank[7][:, 128:256], op0=ALU.mult, op1=ALU.add), reads=[K("S", sc), K("cd"), K("ps_S")], writes=[K("S", sn)])
            g += 1
        S.op('act', lambda e: e.activation(out=t1[:], in_=bank[6][:], func=AF.Square), reads=[BK(6)], writes=[K("t1")])
        S.op('pe', lambda e: e.matmul(bank[5][:], lhsT=ones[:], rhs=t1[:], start=True, stop=True), reads=[K("ones"), K("t1")], writes=[BK(5)])
        S.op('dve', lambda e: e.tensor_scalar(out=t1[:], in0=bank[5][:], scalar1=1.0 / 128.0, scalar2=EPS, op0=ALU.mult, op1=ALU.add), reads=[BK(5)], writes=[K("t1")])
        S.op('act', lambda e: e.activation(out=t1[:], in_=t1[:], func=AF.Sqrt), reads=[K("t1")], writes=[K("t1")])
        S.op('dve', lambda e: e.reciprocal(out=t1[:], in_=t1[:]), reads=[K("t1")], writes=[K("t1")])
        S.op('dve', lambda e: e.scalar_tensor_tensor(out=t1[:], in0=bank[6][:], scalar=pp[:, 12:13], in1=t1[:], op0=ALU.mult, op1=ALU.mult), reads=[BK(6), K("pp"), K("t1")], writes=[K("t1")])
        S.op('act', lambda e: e.activation(out=t2[:], in_=zin[b][:], func=AF.Silu), reads=[K("zin", b)], writes=[K("t2")])
        S.op('dve', lambda e: e.tensor_tensor(out=yo[b][:], in0=t1[:], in1=t2[:], op=ALU.mult), reads=[K("t1"), K("t2")], writes=[K("yo", b)])
        S.dma('sp', io['y'][:, c0:c0 + TT], yo[b][:], reads=[K("yo", b)])


EPS = 1e-6
TS = 512


class Gemm:
    def __init__(self, S, nc, es, KC=8, NS=6, NB=8):
        self.S, self.KC, self.NS, self.NB = S, KC, NS, NB
        self.st = [es.enter_context(nc.sbuf_tensor("g_st%d" % i, [128, KC, 128], F32)) for i in range(NS)]
        self.bf = [es.enter_context(nc.sbuf_tensor("g_bf%d" % i, [128, KC, 128], BF16)) for i in range(NB)]
        self.si = 0
        self.bi = 0
        self.ci = 0
        self.cast_engs = ['dve', 'act']

    def tile(self, Wn, KT, mov, ps_ap, ps_key):
        S, KC = self.S, self.KC
        for kc0 in range(0, KT, KC):
            kc = min(KC, KT - kc0)
            sb = self.si % self.NS; self.si += 1
            bb = self.bi % self.NB; self.bi += 1
            st, bf = self.st[sb], self.bf[bb]
            S.dma('sp', st[:, 0:kc, :], Wn[:, kc0:kc0 + kc, :], writes=[("wst", sb)])
            eng = self.cast_engs[self.ci % len(self.cast_engs)]; self.ci += 1
            if eng == 'act':
                S.op('act', lambda e: e.copy(out=bf[:, 0:kc, :], in_=st[:, 0:kc, :]), reads=[("wst", sb)], writes=[("wbf", bb)])
            else:
                S.op(eng, lambda e: e.tensor_copy(out=bf[:, 0:kc, :], in_=st[:, 0:kc, :]), reads=[("wst", sb)], writes=[("wbf", bb)])
            for k in range(kc):
                kt = kc0 + k
                ap, key = mov(kt)
                S.op('pe', lambda e: e.matmul(ps_ap, lhsT=bf[:, k, :], rhs=ap, start=(kt == 0), stop=(kt == KT - 1)),
                     reads=[("wbf", bb), key], writes=[ps_key], inc=(k == kc - 1))


class Norm:
    def __init__(self, S, nc, es, ones, ps_ss):
        self.S = S
        self.ones = ones
        self.ps = ps_ss
        self.sq = [es.enter_context(nc.sbuf_tensor("n_sq%d" % i, [128, TS], F32)) for i in range(2)]
        self.rs = es.enter_context(nc.sbuf_tensor("n_rs", [128, TS], F32))
        self.i = 0

    def add(self, kt, KT, src_ap, src_key):
        S = self.S
        b = self.i % 2; self.i += 1
        sq = self.sq[b]
        S.op('act', lambda e: e.activation(out=sq[:], in_=src_ap, func=AF.Square), reads=[src_key], writes=[("n_sq", b)])
        S.op('pe', lambda e: e.matmul(self.ps[:], lhsT=self.ones[:], rhs=sq[:], start=(kt == 0), stop=(kt == KT - 1)),
             reads=[("ones",), ("n_sq", b)], writes=[("ps_ss",)])

    def finish(self, D):
        S, rs = self.S, self.rs
        S.op('dve', lambda e: e.tensor_scalar(out=rs[:], in0=self.ps[:], scalar1=1.0 / D, scalar2=EPS, op0=ALU.mult, op1=ALU.add), reads=[("ps_ss",)], writes=[("n_rs",)])
        S.op('act', lambda e: e.activation(out=rs[:], in_=rs[:], func=AF.Sqrt), reads=[("n_rs",)], writes=[("n_rs",)])
        S.op('dve', lambda e: e.reciprocal(out=rs[:], in_=rs[:]), reads=[("n_rs",)], writes=[("n_rs",)])

    def apply(self, out_ap, out_key, src_ap, src_key, gcol, eng='dve'):
        self.S.op('dve', lambda e: e.scalar_tensor_tensor(out=out_ap, in0=src_ap, scalar=gcol, in1=self.rs[:], op0=ALU.mult, op1=ALU.mult),
                  reads=[src_key, ("n_rs",), ("vec",)], writes=[out_key])


def build_A(NTOK, NT_OUT, KT=32):
    nc = bass.Bass("TRN2", target_bir_lowering=False)
    D = lambda n, s, k="ExternalInput": nc.dram_tensor(n, s, F32, kind=k).ap()
    xT = D("xT", [KT * 128, NTOK]); gm = D("gm", [128, KT]); WA = D("WA", [NT_OUT, 128, KT, 128]); ones_d = D("ones", [128, 128])
    uT = D("uT", [NT_OUT * 128, NTOK], "ExternalOutput")
    xv = xT.rearrange("(kt p) t -> p kt t", p=128)
    with ExitStack() as es:
        S = Sch(nc, es)
        A = lambda name, shape, dt=F32: es.enter_context(nc.sbuf_tensor(name, shape, dt))
        ones = A("ones_t", [128, 128]); gmt = A("gmt", [128, KT]); hT = A("hT", [128, KT, TS], BF16)
        xt = [A("xt%d" % i, [128, TS]) for i in range(4)]
        ot = [A("ot%d" % i, [128, TS]) for i in range(3)]
        ps_ss = es.enter_context(nc.psum_tensor("ps_ss", [128, TS], F32))
        ps = [es.enter_context(nc.psum_tensor("ps%d" % i, [128, TS], F32)) for i in range(6)]
        S.dma('sp', ones[:], ones_d, writes=[("ones",)])
        S.dma('sp', gmt[:], gm, writes=[("vec",)])
        G = Gemm(S, nc, es)
        N_ = Norm(S, nc, es, ones, ps_ss)
        xi = 0; oi = 0; pi = 0
        for p in range(NTOK // TS):
            ts = slice(p * TS, (p + 1) * TS)
            for kt in range(KT):
                r = xi % 4; xi += 1
                S.dma('sp', xt[r][:], xv[:, kt, ts], writes=[("xt", r)])
                N_.add(kt, KT, xt[r][:], ("xt", r))
            N_.finish(KT * 128.0)
            for kt in range(KT):
                r = xi % 4; xi += 1
                S.dma('sp', xt[r][:], xv[:, kt, ts], writes=[("xt", r)])
                N_.apply(hT[:, kt, :], ("hT", kt), xt[r][:], ("xt", r), gmt[:, kt:kt + 1])
            for n in range(NT_OUT):
                pb = pi % 6; pi += 1
                G.tile(WA[n], KT, lambda kt: (hT[:, kt, :], ("hT", kt)), ps[pb][:], ("ps", pb))
                ob = oi % 3; oi += 1
                S.op('act', lambda e: e.copy(out=ot[ob][:], in_=ps[pb][:]), reads=[("ps", pb)], writes=[("ot", ob)])
                S.dma('act', uT[n * 128:(n + 1) * 128, ts], ot[ob][:], reads=[("ot", ob)])
        S.finish()
        print("A ops", S.nops, "waits", S.nwaits)
    return nc


def build_C(NTOK, final, KT=32, NB=4, KY=8, FT=86):
    nc = bass.Bass("TRN2", target_bir_lowering=False)
    D = lambda n, s, k="ExternalInput": nc.dram_tensor(n, s, F32, kind=k).ap()
    xT = D("xT", [KT * 128, NTOK]); yT = D("yT", [NB * KY * 128, NTOK])
    vec = D("vec", [128, 3 * KT + NB * KT])
    Wg = D("Wg", [NB * KT, 128, KT, 128]); Pb = D("Pb", [NB * KT, 128, KY, 128]); Wo = D("Wo", [KT, 128, KT, 128])
    Wfg = D("Wfg", [FT, 128, KT, 128]); Wfu = D("Wfu", [FT, 128, KT, 128]); Wd = D("Wd", [KT, 128, FT, 128]); ones_d = D("ones", [128, 128])
    out = D("out", [KT * 128, NTOK], "ExternalOutput")
    xv = xT.rearrange("(kt p) t -> p kt t", p=128)
    yv = yT.rearrange("(kt p) t -> p kt t", p=128)
    ov = out.rearrange("(kt p) t -> p kt t", p=128)
    with ExitStack() as es:
        S = Sch(nc, es)
        A = lambda name, shape, dt=F32: es.enter_context(nc.sbuf_tensor(name, shape, dt))
        ones = A("ones_t", [128, 128]); vt = A("vt", [128, 3 * KT + NB * KT])
        hT = A("hT", [128, KT, TS], BF16); big = A("big", [128, max(FT, NB * KY + KT), TS], BF16)
        yTb = lambda kt: big[:, kt, :]
        mT = lambda kt: big[:, NB * KY + kt, :]
        xt = [A("xt%d" % i, [128, TS]) for i in range(4)]
        gt = [A("gt%d" % i, [128, TS]) for i in range(2)]
        macc = A("macc", [128, TS]); tmp = A("tmp", [128, TS])
        ot = [A("ot%d" % i, [128, TS]) for i in range(3)]
        ps_ss = es.enter_context(nc.psum_tensor("ps_ss", [128, TS], F32))
        ps = [es.enter_context(nc.psum_tensor("ps%d" % i, [128, TS], F32)) for i in range(7)]
        S.dma('sp', ones[:], ones_d, writes=[("ones",)])
        S.dma('sp', vt[:], vec, writes=[("vec",)])
        G = Gemm(S, nc, es)
        N_ = Norm(S, nc, es, ones, ps_ss)
        st = dict(xi=0, oi=0, pi=0, gi=0)

        def nxt(k, m):
            v = st[k] % m; st[k] += 1
            return v
        for p in range(NTOK // TS):
            ts = slice(p * TS, (p + 1) * TS)
            for kt in range(KT):
                r = nxt('xi', 4)
                S.dma('act', xt[r][:], xv[:, kt, ts], writes=[("xt", r)])
                N_.add(kt, KT, xt[r][:], ("xt", r))
            N_.finish(KT * 128.0)
            for kt in range(KT):
                r = nxt('xi', 4)
                S.dma('act', xt[r][:], xv[:, kt, ts], writes=[("xt", r)])
                N_.apply(hT[:, kt, :], ("hT", kt), xt[r][:], ("xt", r), vt[:, kt:kt + 1])
            for kt in range(NB * KY):
                r = nxt('xi', 4)
                S.dma('act', xt[r][:], yv[:, kt, ts], writes=[("xt", r)])
                S.op('pool', lambda e: e.tensor_copy(out=yTb(kt), in_=xt[r][:]), reads=[("xt", r)], writes=[("big", kt)])
            for c in range(KT):
                for b in range(NB):
                    pg = nxt('pi', 7)
                    G.tile(Wg[b * KT + c], KT, lambda kt: (hT[:, kt, :], ("hT", kt)), ps[pg][:], ("ps", pg))
                    gi = nxt('gi', 2)
                    bcol = 3 * KT + b * KT + c
                    S.op('act', lambda e: e.activation(out=gt[gi][:], in_=ps[pg][:], func=AF.Sigmoid, bias=vt[:, bcol:bcol + 1]), reads=[("ps", pg), ("vec",)], writes=[("gt", gi)])
                    pz = nxt('pi', 7)
                    G.tile(Pb[b * KT + c], KY, lambda kt: (yTb(b * KY + kt), ("big", b * KY + kt)), ps[pz][:], ("ps", pz))
                    if b == 0:
                        S.op('dve', lambda e: e.tensor_tensor(out=macc[:], in0=ps[pz][:], in1=gt[gi][:], op=ALU.mult), reads=[("ps", pz), ("gt", gi)], writes=[("macc",)])
                    else:
                        S.op('dve', lambda e: e.tensor_tensor(out=tmp[:], in0=ps[pz][:], in1=gt[gi][:], op=ALU.mult), reads=[("ps", pz), ("gt", gi)], writes=[("tmp",)])
                        if b < NB - 1:
                            S.op('pool', lambda e: e.tensor_tensor(out=macc[:], in0=macc[:], in1=tmp[:], op=ALU.add), reads=[("macc",), ("tmp",)], writes=[("macc",)])
                        else:
                            S.op('pool', lambda e: e.tensor_tensor(out=mT(c), in0=macc[:], in1=tmp[:], op=ALU.add), reads=[("macc",), ("tmp",)], writes=[("big", NB * KY + c)])
            for c in range(KT):
                pg = nxt('pi', 7)
                G.tile(Wo[c], KT, lambda kt: (mT(kt), ("big", NB * KY + kt)), ps[pg][:], ("ps", pg))
                r = nxt('xi', 4)
                S.dma('act', xt[r][:], xv[:, c, ts], writes=[("xt", r)])
                ob = nxt('oi', 3)
                S.op('dve', lambda e: e.tensor_tensor(out=ot[ob][:], in0=ps[pg][:], in1=xt[r][:], op=ALU.add), reads=[("ps", pg), ("xt", r)], writes=[("ot", ob)])
                S.dma('act', ov[:, c, ts], ot[ob][:], reads=[("ot", ob)], writes=[("out", c)])
                N_.add(c, KT, ot[ob][:], ("ot", ob))
            N_.finish(KT * 128.0)
            for kt in range(KT):
                r = nxt('xi', 4)
                S.dma('act', xt[r][:], ov[:, kt, ts], reads=[("out", kt)], writes=[("xt", r)])
                N_.apply(hT[:, kt, :], ("hT", kt), xt[r][:], ("xt", r), vt[:, KT + kt:KT + kt + 1])
            for f in range(FT):
                pa = nxt('pi', 7)
                G.tile(Wfg[f], KT, lambda kt: (hT[:, kt, :], ("hT", kt)), ps[pa][:], ("ps", pa))
                gi = nxt('gi', 2)
                S.op('act', lambda e: e.activation(out=gt[gi][:], in_=ps[pa][:], func=AF.Silu), reads=[("ps", pa)], writes=[("gt", gi)])
                pu = nxt('pi', 7)
                G.tile(Wfu[f], KT, lambda kt: (hT[:, kt, :], ("hT", kt)), ps[pu][:], ("ps", pu))
                S.op('dve', lambda e: e.tensor_tensor(out=big[:, f, :], in0=ps[pu][:], in1=gt[gi][:], op=ALU.mult), reads=[("ps", pu), ("gt", gi)], writes=[("big", f)])
            for c in range(KT):
                pg = nxt('pi', 7)
                G.tile(Wd[c], FT, lambda kt: (big[:, kt, :], ("big", kt)), ps[pg][:], ("ps", pg))
                r = nxt('xi', 4)
                S.dma('act', xt[r][:], ov[:, c, ts], reads=[("out", c)], writes=[("xt", r)])
                ob = nxt('oi', 3)
                S.op('dve', lambda e: e.tensor_tensor(out=ot[ob][:], in0=ps[pg][:], in1=xt[r][:], op=ALU.add), reads=[("ps", pg), ("xt", r)], writes=[("ot", ob)])
                S.dma('act', ov[:, c, ts], ot[ob][:], reads=[("ot", ob)], writes=[("out", c)])
                if final:
                    N_.add(c, KT, ot[ob][:], ("ot", ob))
            if final:
                N_.finish(KT * 128.0)
                for kt in range(KT):
                    r = nxt('xi', 4)
                    S.dma('act', xt[r][:], ov[:, kt, ts], reads=[("out", kt)], writes=[("xt", r)])
                    ob = nxt('oi', 3)
                    N_.apply(ot[ob][:], ("ot", ob), xt[r][:], ("xt", r), vt[:, 2 * KT + kt:2 * KT + kt + 1])
                    S.dma('act', ov[:, kt, ts], ot[ob][:], reads=[("ot", ob)], writes=[("out", kt)])
        S.finish()
        print("C ops", S.nops, "waits", S.nwaits)
    return nc


def tile_w(W):
    K, N = W.shape
    NP = ((N + 127) // 128) * 128
    if NP != N:
        W = np.concatenate([W, np.zeros((K, NP - N), W.dtype)], axis=1)
    return np.ascontiguousarray(W.reshape(K // 128, 128, NP // 128, 128).transpose(2, 1, 0, 3))


def build_B(T):
    nc = bass.Bass("TRN2", target_bir_lowering=False)
    D = lambda n, s, k="ExternalInput": nc.dram_tensor(n, s, F32, kind=k).ap()
    NC_ = T // 64
    cst = dict(ident=D("c_ident", [128, 128]), ones=D("c_ones", [128, 128]), reset=D("c_reset", [128, 512]))
    io_lru = dict(lx=D("l_x", [128, T]), lg=D("l_g", [128, T]), pp=D("l_pp", [128, 16]), wr=D("l_wr", [128, 128]), wi=D("l_wi", [128, 128]),
                  y=D("y_lru", [128, T], "ExternalOutput"))
    io_hg = dict(q=D("h_q", [128, T]), f=D("h_f", [128, T]), g=D("h_g", [128, T]), v=D("h_v", [64, NC_, 128]), pp=D("h_pp", [128, 8]),
                 maskc=D("h_maskc", [64, 512]), y=D("y_hg", [128, T], "ExternalOutput"), **cst)
    io_ret = dict(q=D("r_q", [128, 2, T]), k=D("r_k", [128, 2, T]), g=D("r_g", [128, 2, T]), v=D("r_v", [64, NC_, 256]), pp=D("r_pp", [128, 4]),
                  cos=D("r_cos", [128, T]), sin=D("r_sin", [128, T]), qdec=D("r_qdec", [128, 512]), kdec=D("r_kdec", [128, 512]), dmask=D("r_dmask", [64, 512]),
                  ident=cst["ident"], ones=cst["ones"], y=D("y_ret", [128, 2, T], "ExternalOutput"))
    io_dn = dict(q=D("d_q", [128, T]), k=D("d_k", [128, T]), v=D("d_v", [128, T]), z=D("d_z", [128, T]), ab_b=D("d_abb", [128, 2, T]), ab_t=D("d_abt", [64, 2, NC_]),
                 pp=D("d_pp", [128, 16]), cneg=D("d_cneg", [64, 512]), lneg=D("d_lneg", [64, 512]), ustrict=D("d_ustrict", [64, 512]), tri=D("d_tri", [64, 64]),
                 eye8=D("d_eye8", [64, 512]), y=D("y_dn", [128, T], "ExternalOutput"), **cst)
    with ExitStack() as es:
        S = Sch(nc, es)
        for emit, io in ((emit_dn, io_dn), (emit_ret, io_ret), (emit_hg, io_hg), (emit_lru, io_lru)):
            with ExitStack() as es2:
                emit(S, nc, es2, T, io)
                S.barrier()
        S.finish()
        print("B ops", S.nops, "waits", S.nwaits)
    return nc


D_MODEL = 4096
SEQ = 16384
NCORE = 8
NTOK = SEQ // NCORE
D_MIX = 1024
N_MIXCOLS = 14 * D_MIX + 16
_CACHE = {}


def _prog(name, fn):
    if name not in _CACHE:
        _CACHE[name] = fn()
    return _CACHE[name]


def _run(nc, in_maps):
    res = run_bass_kernel_spmd(nc, in_maps, core_ids=list(range(NCORE)))
    return res.results


def _vtile(v):
    return np.ascontiguousarray(np.asarray(v, np.float32).reshape(-1, 128).T)


def _tok_major(a, width):
    T = a.shape[1]
    return np.ascontiguousarray(a.T.reshape(T // 64, 64, width).transpose(1, 0, 2))


def kernel(**inputs):
    f32 = np.float32
    x = np.asarray(inputs["x"], f32)[0]
    T = SEQ
    xT = np.ascontiguousarray(x.T)
    ones = np.ones((128, 128), f32)
    hgc = hg_consts()
    dnc = dn_consts()
    retc = [ret_consts(h, T) for h in range(4)]
    norm_mix = np.asarray(inputs["norm_mix"], f32); norm_ffn = np.asarray(inputs["norm_ffn"], f32); norm_final = np.asarray(inputs["norm_final"], f32)
    merge_bias = np.asarray(inputs["merge_bias"], f32)
    for l in range(2):
        w_in = np.asarray(inputs["w_in"][l], f32)
        WA = tile_w(w_in[:, :N_MIXCOLS])
        NT_A = WA.shape[0]
        ncA = _prog("A", lambda: build_A(NTOK, NT_A))
        gm = _vtile(norm_mix[l])
        maps = [dict(xT=np.ascontiguousarray(xT[:, c * NTOK:(c + 1) * NTOK]), gm=gm, WA=WA, ones=ones) for c in range(NCORE)]
        resA = _run(ncA, maps)
        del WA, maps
        uT = np.concatenate([r["uT"] for r in resA], axis=1)
        del resA
        blk = lambda j, c, w=128: uT[j * D_MIX + c * w: j * D_MIX + (c + 1) * w, :]
        ncB = _prog("B", lambda: build_B(T))
        cw = np.asarray(inputs["lru_conv_w"][l], f32); dcw = np.asarray(inputs["dn_conv_w"][l], f32)
        maps = []
        for c in range(NCORE):
            rh = c // 2
            sl = slice(c * 128, (c + 1) * 128)
            m = dict(c_ident=np.eye(128, dtype=f32), c_ones=ones, c_reset=hgc["reset"])
            lpp = np.zeros((128, 16), f32)
            lpp[:, 0:4] = cw[:, sl].T; lpp[:, 4] = inputs["lru_conv_b"][l][sl]; lpp[:, 5] = inputs["lru_b_r"][l][sl]
            lpp[:, 6] = inputs["lru_b_i"][l][sl]; lpp[:, 7] = inputs["lru_a"][l][sl]
            m.update(l_x=np.ascontiguousarray(blk(4, c)), l_g=np.ascontiguousarray(blk(5, c)), l_pp=lpp,
                     l_wr=np.ascontiguousarray(np.asarray(inputs["lru_w_r"][l][c], f32)), l_wi=np.ascontiguousarray(np.asarray(inputs["lru_w_i"][l][c], f32)))
            hpp = np.zeros((128, 8), f32)
            hpp[:, 0] = inputs["hg_lb_logits"][0][sl]; hpp[:, 1] = inputs["hg_lb_logits"][1][sl]; hpp[:, 2] = float(l); hpp[:, 3] = inputs["hg_norm"][l][sl]
            m.update(h_q=np.ascontiguousarray(blk(6, c)), h_f=np.ascontiguousarray(blk(7, c)), h_g=np.ascontiguousarray(blk(9, c)),
                     h_v=_tok_major(blk(8, c), 128), h_pp=hpp, h_maskc=hgc["maskc"])
            rc, g64 = retc[rh]
            rpp = np.zeros((128, 4), f32)
            gn = np.asarray(inputs["ret_gn"][l], f32)[rh * 256:(rh + 1) * 256]
            rpp[:, 0] = g64; rpp[:, 1] = gn[:128]; rpp[:, 2] = gn[128:]
            fm2 = lambda j: np.ascontiguousarray(blk(j, rh, 256).reshape(2, 128, T).transpose(1, 0, 2))
            m.update(r_q=fm2(0), r_k=fm2(1), r_g=fm2(3), r_v=_tok_major(blk(2, rh, 256), 256), r_pp=rpp,
                     r_cos=rc["cos"], r_sin=rc["sin"], r_qdec=rc["qdec"], r_kdec=rc["kdec"], r_dmask=rc["dmask"])
            dpp = np.zeros((128, 16), f32)
            dpp[:, 0:4] = dcw[:, c * 128:(c + 1) * 128].T; dpp[:, 4:8] = dcw[:, 1024 + c * 128:1024 + (c + 1) * 128].T
            dpp[:, 8:12] = dcw[:, 2048 + c * 128:2048 + (c + 1) * 128].T
            dpp[:, 12] = inputs["dn_norm"][l]; dpp[:, 13] = inputs["dn_a_log"][l][c]; dpp[:, 14] = inputs["dn_dt_bias"][l][c]
            ab = np.stack([uT[14 * D_MIX + c, :], uT[14 * D_MIX + 8 + c, :]], 0)
            m.update(d_q=np.ascontiguousarray(blk(10, c)), d_k=np.ascontiguousarray(blk(11, c)), d_v=np.ascontiguousarray(blk(12, c)), d_z=np.ascontiguousarray(blk(13, c)),
                     d_abb=np.ascontiguousarray(np.broadcast_to(ab[None], (128, 2, T))), d_abt=np.ascontiguousarray(ab.reshape(2, T // 64, 64).transpose(2, 0, 1)),
                     d_pp=dpp, d_cneg=dnc["cneg"], d_lneg=dnc["lneg"], d_ustrict=dnc["ustrict"], d_tri=dnc["tri"], d_eye8=dnc["eye8"])
            maps.append(m)
        del uT
        resB = _run(ncB, maps)
        del maps
        yT = np.empty((4 * D_MIX, T), f32)
        for c in range(NCORE):
            r = resB[c]
            if c % 2 == 0:
                rh = c // 2
                yT[rh * 256:(rh + 1) * 256, :] = r["y_ret"].transpose(1, 0, 2).reshape(256, T)
            yT[1 * D_MIX + c * 128:1 * D_MIX + (c + 1) * 128, :] = r["y_lru"]
            yT[2 * D_MIX + c * 128:2 * D_MIX + (c + 1) * 128, :] = r["y_hg"]
            yT[3 * D_MIX + c * 128:3 * D_MIX + (c + 1) * 128, :] = r["y_dn"]
        del resB
        final = (l == 1)
        ncC = _prog("C%d" % int(final), lambda: build_C(NTOK, final))
        vec = np.ascontiguousarray(np.concatenate([_vtile(norm_mix[l]), _vtile(norm_ffn[l]), _vtile(norm_final), _vtile(merge_bias[l])], axis=1))
        Wg = tile_w(w_in[:, N_MIXCOLS:])
        del w_in
        wb = np.asarray(inputs["w_branch"][l], f32)
        Pb = np.concatenate([tile_w(wb[b]) for b in range(4)], 0)
        Wo = tile_w(np.asarray(inputs["w_out"][l], f32))
        Wfg = tile_w(np.asarray(inputs["w_ffn_gate"][l], f32)); Wfu = tile_w(np.asarray(inputs["w_ffn_up"][l], f32)); Wd = tile_w(np.asarray(inputs["w_ffn_down"][l], f32))
        maps = [dict(xT=np.ascontiguousarray(xT[:, c * NTOK:(c + 1) * NTOK]), yT=np.ascontiguousarray(yT[:, c * NTOK:(c + 1) * NTOK]), vec=vec,
                     Wg=Wg, Pb=Pb, Wo=Wo, Wfg=Wfg, Wfu=Wfu, Wd=Wd, ones=ones) for c in range(NCORE)]
        del yT
        resC = _run(ncC, maps)
        del maps, Wg, Pb, Wo, Wfg, Wfu, Wd
        xT = np.concatenate([r["out"] for r in resC], axis=1)
        del resC
    return np.ascontiguousarray(xT.T)[None].astype(f32)
```
